# Optimizing a Trainium2 kernel written in Bass

```python
import math
import jax, jax.numpy as jnp
from jax import lax
import numpy as np

D_MODEL = 1024
BATCH = 2
SEQ = 8192
DEPTH = 1

MIX_WIDTH = D_MODEL
WIDTH_A = MIX_WIDTH // 2
WIDTH_B = MIX_WIDTH - WIDTH_A
DIFF_HEAD_DIM = 64
DIFF_V_DIM = 2 * DIFF_HEAD_DIM
N_HEADS_A = WIDTH_A // DIFF_V_DIM
HEAD_DIM_B = 64
N_HEADS_B = WIDTH_B // HEAD_DIM_B
N_KV_B = 2
GQA_GROUP = N_HEADS_B // N_KV_B
GRID_W = 64
ROPE_THETA = 10000.0
ROT_HALF = HEAD_DIM_B // 2
Q_BLOCK = 128
NORM_EPS = 1e-6
PROJ_SIZES = (
    N_HEADS_A * 2 * DIFF_HEAD_DIM,
    N_HEADS_A * 2 * DIFF_HEAD_DIM,
    WIDTH_A,
    WIDTH_A,
    N_HEADS_B * HEAD_DIM_B,
    N_KV_B * HEAD_DIM_B,
    N_KV_B * HEAD_DIM_B,
    WIDTH_B,
)
PROJ_OUT = sum(PROJ_SIZES)
PROJ_SPLITS = tuple(int(v) for v in np.cumsum(PROJ_SIZES)[:-1])

kernel_name = "hybrid_diffattn_gqa_axialrope_adaln"


def rmsnorm(x, w):
    xf = x.astype(jnp.float32)
    y = xf * lax.rsqrt(jnp.mean(xf * xf, axis=-1, keepdims=True) + NORM_EPS)
    return (y * w.astype(jnp.float32)).astype(x.dtype)


def lambda_init_for(layer_idx):
    return 0.8 - 0.6 * math.exp(-0.3 * layer_idx)


def alibi_slopes(n_heads):
    s = 2.0 ** (-8.0 * np.arange(1, n_heads + 1) / n_heads)
    return jnp.asarray(s, dtype=jnp.float32)


def axial_rope_tables(rows):
    row = jnp.repeat(jnp.arange(rows), GRID_W).astype(jnp.float32)
    col = jnp.tile(jnp.arange(GRID_W), rows).astype(jnp.float32)
    freqs = 1.0 / (ROPE_THETA ** (jnp.arange(ROT_HALF // 2, dtype=jnp.float32) * 2.0 / ROT_HALF))
    ang_r = row[:, None] * freqs[None, :]
    ang_c = col[:, None] * freqs[None, :]
    return jnp.cos(ang_r), jnp.sin(ang_r), jnp.cos(ang_c), jnp.sin(ang_c)


def rotate(x, cos, sin):
    x1, x2 = jnp.split(x, 2, axis=-1)
    cos = cos.astype(x.dtype)
    sin = sin.astype(x.dtype)
    return jnp.concatenate([x1 * cos - x2 * sin, x2 * cos + x1 * sin], axis=-1)


def axial_rope(x, tables):
    cos_r, sin_r, cos_c, sin_c = tables
    xr, xc = jnp.split(x, 2, axis=-1)
    return jnp.concatenate([rotate(xr, cos_r, sin_r), rotate(xc, cos_c, sin_c)], axis=-1)


def hybrid_layer(x, c, layer_idx, rope_tables, slopes, w_ada, b_ada, norm_w, w_in,
                 lq1, lk1, lq2, lk2, subln_w, q_norm_w, k_norm_w, w_out):
    bsz, seq, _ = x.shape
    nblk = seq // Q_BLOCK
    lam_init = lambda_init_for(layer_idx)

    mod = jax.nn.silu(c) @ w_ada + b_ada
    shift, scale, gate = jnp.split(mod, 3, axis=-1)
    h = rmsnorm(x, norm_w) * (1.0 + scale[:, None, :]) + shift[:, None, :]

    proj = h @ w_in
    qa, ka, va, ga, qb, kb, vb, gb = jnp.split(proj, PROJ_SPLITS, axis=-1)

    qa = qa.reshape(bsz, seq, N_HEADS_A, 2, DIFF_HEAD_DIM).transpose(0, 2, 3, 1, 4)
    ka = ka.reshape(bsz, seq, N_HEADS_A, 2, DIFF_HEAD_DIM).transpose(0, 2, 3, 1, 4)
    va = va.reshape(bsz, seq, N_HEADS_A, DIFF_V_DIM).transpose(0, 2, 1, 3)
    lam = (jnp.exp(jnp.sum(lq1.astype(jnp.float32) * lk1.astype(jnp.float32)))
           - jnp.exp(jnp.sum(lq2.astype(jnp.float32) * lk2.astype(jnp.float32)))
           + lam_init)

    qb = qb.reshape(bsz, seq, N_KV_B, GQA_GROUP, HEAD_DIM_B).transpose(0, 2, 3, 1, 4)
    kb = kb.reshape(bsz, seq, N_KV_B, HEAD_DIM_B).transpose(0, 2, 1, 3)
    vb = vb.reshape(bsz, seq, N_KV_B, HEAD_DIM_B).transpose(0, 2, 1, 3)
    qb = axial_rope(rmsnorm(qb, q_norm_w), rope_tables)
    kb = axial_rope(rmsnorm(kb, k_norm_w), rope_tables)

    qa_blk = jnp.moveaxis(qa.reshape(bsz, N_HEADS_A, 2, nblk, Q_BLOCK, DIFF_HEAD_DIM), 3, 0)
    qb_blk = jnp.moveaxis(qb.reshape(bsz, N_KV_B, GQA_GROUP, nblk, Q_BLOCK, HEAD_DIM_B), 3, 0)
    kpos = jnp.arange(seq, dtype=jnp.float32)
    scale_a = 1.0 / math.sqrt(DIFF_HEAD_DIM)
    scale_b = 1.0 / math.sqrt(HEAD_DIM_B)

    def query_block(args):
        qa_i, qb_i, i = args
        qpos = (i * Q_BLOCK + jnp.arange(Q_BLOCK)).astype(jnp.float32)
        dist = jnp.abs(qpos[:, None] - kpos[None, :])
        sa = (jnp.einsum('bhmqd,bhmkd->bhmqk', qa_i, ka).astype(jnp.float32) * scale_a
              - slopes[:, None, None, None] * dist)
        pa = jax.nn.softmax(sa, axis=-1)
        diff = pa[:, :, 0] - lam * pa[:, :, 1]
        oa = jnp.einsum('bhqk,bhkd->bhqd', diff.astype(va.dtype), va)
        sb = jnp.einsum('bgrqd,bgkd->bgrqk', qb_i, kb).astype(jnp.float32) * scale_b
        pb = jax.nn.softmax(sb, axis=-1)
        ob = jnp.einsum('bgrqk,bgkd->bgrqd', pb.astype(vb.dtype), vb)
        return oa, ob

    oa, ob = lax.map(query_block, (qa_blk, qb_blk, jnp.arange(nblk)))
    oa = oa.transpose(1, 0, 3, 2, 4).reshape(bsz, seq, N_HEADS_A, DIFF_V_DIM)
    oa = (rmsnorm(oa, subln_w) * (1.0 - lam_init)).reshape(bsz, seq, WIDTH_A)
    ob = ob.transpose(1, 0, 4, 2, 3, 5).reshape(bsz, seq, WIDTH_B)

    y = jnp.concatenate([oa * jax.nn.silu(ga), ob * jax.nn.silu(gb)], axis=-1) @ w_out
    return x + gate[:, None, :] * y


def setup_inputs(seed: int = 0) -> dict:
    key = jax.random.key(seed)
    ks = jax.random.split(key, 16)
    f32 = jnp.float32
    D = D_MODEL
    nrm = lambda k, shape, s: jax.random.normal(k, shape, f32) * s
    return {
        "x": nrm(ks[0], (BATCH, SEQ, D), 1.0),
        "c": nrm(ks[1], (BATCH, D), 1.0),
        "w_ada": nrm(ks[2], (DEPTH, D, 3 * D), 0.5 * D ** -0.5),
        "b_ada": nrm(ks[3], (DEPTH, 3 * D), 0.02),
        "norm_w": 1.0 + nrm(ks[4], (DEPTH, D), 0.02),
        "w_in": nrm(ks[5], (DEPTH, D, PROJ_OUT), D ** -0.5),
        "lambda_q1": nrm(ks[6], (DEPTH, DIFF_HEAD_DIM), 0.1),
        "lambda_k1": nrm(ks[7], (DEPTH, DIFF_HEAD_DIM), 0.1),
        "lambda_q2": nrm(ks[8], (DEPTH, DIFF_HEAD_DIM), 0.1),
        "lambda_k2": nrm(ks[9], (DEPTH, DIFF_HEAD_DIM), 0.1),
        "subln_w": 1.0 + nrm(ks[10], (DEPTH, DIFF_V_DIM), 0.02),
        "q_norm_w": 1.0 + nrm(ks[11], (DEPTH, HEAD_DIM_B), 0.02),
        "k_norm_w": 1.0 + nrm(ks[12], (DEPTH, HEAD_DIM_B), 0.02),
        "w_out": nrm(ks[13], (DEPTH, MIX_WIDTH, D), MIX_WIDTH ** -0.5),
        "final_norm_w": 1.0 + nrm(ks[14], (D,), 0.02),
    }


def reference(x, c, w_ada, b_ada, norm_w, w_in, lambda_q1, lambda_k1, lambda_q2, lambda_k2,
              subln_w, q_norm_w, k_norm_w, w_out, final_norm_w):
    seq = x.shape[1]
    rows = seq // GRID_W
    rope_tables = axial_rope_tables(rows)
    slopes = alibi_slopes(N_HEADS_A)
    for l in range(DEPTH):
        x = hybrid_layer(x, c, l, rope_tables, slopes, w_ada[l], b_ada[l], norm_w[l], w_in[l],
                         lambda_q1[l], lambda_k1[l], lambda_q2[l], lambda_k2[l], subln_w[l],
                         q_norm_w[l], k_norm_w[l], w_out[l])
    return rmsnorm(x, final_norm_w)
```

```python
import math
import numpy as np
import ml_dtypes
import concourse.bass as bass
import concourse.mybir as mybir
from concourse.bass_utils import run_bass_kernel_spmd

F32 = mybir.dt.float32
BF16 = mybir.dt.bfloat16
ALU = mybir.AluOpType
ACTF = mybir.ActivationFunctionType
AX = mybir.AxisListType

S = 8192
D = 1024
NT = S // 128
NB = S // 512
EPS = 1e-6
LAM_INIT = 0.8 - 0.6 * math.exp(0.0)
ENGS = ["sync", "scalar", "gpsimd", "vector", "tensor"]


class Sched:
    def __init__(self):
        self.ops = {e: [] for e in ENGS}
        self.cnt = {}
        self.ev = {}
        self.chan_eng = {}
        self.marks = {}
        self.seq = []
        self.limit = None

    def mark(self, name):
        self.marks[name] = {e: len(self.ops[e]) for e in ENGS}

    def add(self, eng, fn, waits=(), sig=None, chan=None, dma=False):
        ch = None
        inc = 16 if dma else 1
        if sig is not None:
            ch = chan or ("c_" + eng)
            assert self.chan_eng.setdefault(ch, eng) == eng, (ch, eng)
            self.cnt[ch] = self.cnt.get(ch, 0) + inc
            assert sig not in self.ev, sig
            self.ev[sig] = (ch, self.cnt[ch])
        self.ops[eng].append((fn, tuple(w for w in waits if w is not None), sig, ch, inc))
        self.seq.append(eng)

    def emit(self, ename, eng, sems):
        waited = {}
        if ename == "sync":
            self.pidj = eng.partition_id() % 4
        ops = self.ops[ename]
        if self.limit is not None:
            if self.limit.startswith("n:"):
                ops = ops[:self.seq[:int(self.limit[2:])].count(ename)]
            else:
                ops = ops[:self.marks[self.limit][ename]]
        for fn, waits, sig, ch, inc in ops:
            for w in waits:
                wch, val = self.ev[w]
                if waited.get(wch, 0) >= val:
                    continue
                eng.wait_ge(sems[wch], val)
                waited[wch] = val
            ins = fn(eng)
            if sig is not None:
                ins.then_inc(sems[ch], inc)


def build_program(trunc=None):
    nc = bass.Bass("TRN2", target_bir_lowering=False)

    def din(name, shape, dt=F32):
        return nc.dram_tensor(name, list(shape), dt, kind="ExternalInput").ap()

    x = din("x", [S, D])
    xo = din("xo", [NB * 128, D])
    cT = din("cT", [128, 8])
    w_ada = din("w_ada", [D, 3 * D])
    bada = din("bada", [128, 3 * D])
    normwT = din("normwT", [128, 8])
    w_in = din("w_in", [D, 896])
    w_out = din("w_out", [D, D])
    lam_in = din("lam_in", [128, 256])
    sublnT = din("sublnT", [128, 1])
    wn_in = din("wn_in", [128, 192])
    fnw_in = din("fnw_in", [128, D])
    ident_bf_in = din("ident_bf", [128, 128], BF16)
    ident_f_in = din("ident_f", [128, 128])
    rope_in = din("rope", [NT, 128, 128])
    kaug_in = din("kaug", [4, S], BF16)
    qaugb_in = din("qaugb", [4, S], BF16)
    qauga_in = din("qauga", [4, S], BF16)
    dtile_in = din("dtile", [128, 128], BF16)
    sel_in = din("sel", [64, 128])
    out = nc.dram_tensor("out", [NB * 128, D], F32, kind="ExternalOutput").ap()

    qa_scr = [nc.dram_tensor(f"qa_scr{m}", [64, S], BF16).ap() for m in range(2)]
    qb_scr = nc.dram_tensor("qb_scr", [128, S], BF16).ap()
    sga_scr = nc.dram_tensor("sga_scr", [128, S], BF16).ap()
    sgb_scr = nc.dram_tensor("sgb_scr", [128, S], BF16).ap()
    exin = [nc.dram_tensor(f"exin{i}", [256, 512], BF16) for i in range(NB)]
    exout = [nc.dram_tensor(f"exout{i}", [1024, 512], BF16) for i in range(NB)]

    sch = Sched()
    A = sch.add
    from contextlib import ExitStack
    es = ExitStack()

    def sb(name, shape, dt=F32):
        return es.enter_context(nc.sbuf_tensor("s_" + name, list(shape), dt))

    def ps(name, shape, dt=F32):
        return es.enter_context(nc.psum_tensor("p_" + name, list(shape), dt))

    with es:
        KA = [sb(f"KA{m}", [68, S], BF16) for m in range(2)]
        KB = sb("KB", [128, S], BF16)
        VA = sb("VA", [128, NT, 128], BF16)
        VB = sb("VB", [128, NT, 64], BF16)
        Wout = sb("Wout", [128, 8, D], BF16)
        hT = [sb(f"hT{i}", [128, 8, 512], BF16) for i in range(2)]
        h0f = hT[0][:].rearrange("p a b -> p (a b)").bitcast(F32)
        h1f = hT[1][:].rearrange("p a b -> p (a b)").bitcast(F32)
        mtmp = h0f[:, 0:1024]
        prod = h0f[:, 1024:2048].rearrange("p (a b) -> p a b", a=8)
        badat = h1f[:, 0:1024]
        scb = h1f[:, 1024:2048].rearrange("p (a b) -> p a b", a=8)
        ropet = [sb(f"ropet{i}", [128, 128]) for i in range(3)]
        gate_bc = sb("gate_bc", [128, D])
        fnw = sb("fnw", [128, D])
        small = sb("small", [128, 64])
        sc = sb("sc", [128, 8])
        shiftT = sb("shiftT", [128, 8])
        scaleT = sb("scaleT", [128, 8])
        aT = sb("aT", [128, 8])
        normw_sb = sb("normw_sb", [128, 8])
        cT_sb = sb("cT_sb", [128, 8])
        lam_sb = sb("lam_sb", [128, 256])
        lamp = sb("lamp", [128, 128])
        nlam = sb("nlam", [128, 1])
        subw = sb("subw", [128, 1])
        WN = sb("WN", [128, 192])
        ident_bf = sb("ident_bf", [128, 128], BF16)
        ident_f = sb("ident_f", [128, 128])
        ones_bf = sb("ones_bf", [128, 128], BF16)
        ones_f = sb("ones_f", [128, 128])
        dtile = sb("dtile", [128, 128], BF16)
        sel = sb("sel", [64, 128])
        s64 = sb("s64", [64, 512])
        ss = sb("ss", [128, NT])
        rstd = sb("rstd", [128, NT])
        ss3 = sb("ss3", [128, 3])
        rstd3 = sb("rstd3", [128, 3])
        yo = [sb(f"yo{i}", [128, 256], BF16) for i in range(2)]
        UN = 32768
        U = sb("U", [128, UN], BF16)
        cur = [0]

        def cv(shape, dt=BF16, rows=128):
            n = 1
            for d_ in shape[1:]:
                n *= d_
            nb = n * (2 if dt == F32 else 1)
            o = cur[0]
            cur[0] += nb
            assert cur[0] <= UN, cur[0]
            v = U[0:shape[0], o:o + nb]
            if dt == F32:
                v = v.bitcast(F32)
            if len(shape) == 3:
                v = v.rearrange("p (a b) -> p a b", a=shape[1])
            return v
        xn = [cv([128, D]) for i in range(8)]
        xt = [cv([128, D], F32) for i in range(3)]
        wst = [cv([128, 1024], F32) for i in range(2)]
        Win = cv([128, 8, 896])
        qst = [cv([128, 512]) for i in range(3)]
        gtmp = [cv([128, 512], F32) for i in range(2)]
        QBst = [cv([128, 512]) for i in range(2)]
        tq = cv([128, 192], F32)
        tsq = cv([128, 192], F32)
        ty = cv([128, 192], F32)
        tA = cv([128, 192], F32)
        tB = cv([128, 192], F32)
        print("phase1 union bytes", cur[0] * 2)
        cur[0] = 0
        Qb = [[cv([68, 512]) for m in range(2)] for p in range(2)]
        Qa = [[cv([68, 512]) for m in range(2)] for p in range(2)]
        QBb = [[cv([128, 512]) for h in range(2)] for p in range(2)]
        SGA = [cv([128, 512]) for p in range(2)]
        SGB = [[cv([64, 512]) for h in range(2)] for p in range(2)]
        PT = [cv([128, 1024]) for i in range(3)]
        fR = cv([128, 512], F32)
        fT1 = cv([128, 512], F32)
        fT2 = cv([128, 512], F32)
        fD = cv([128, 512], F32)
        Ocp = cv([128, 512], F32)
        fD2 = fT2
        fRS = fR
        GA = [cv([128, 512]) for p in range(2)]
        GB = [[cv([64, 512]) for h in range(2)] for p in range(2)]
        Gg = [cv([128, 8, 128]) for p in range(2)]
        xot = [cv([128, D], F32) for p in range(2)]
        zt = cv([128, D], F32)
        ot = [cv([128, D], F32) for p in range(2)]
        zss = sb("zss", [128, 2])
        neghalf = sb("neghalf", [128, 512])

        def NH(ap):
            return neghalf[0:ap.shape[0], 0:ap.shape[1]]
        print("phase2 union bytes", cur[0] * 2)

        SP = [ps(f"SP{i}", [128, 1024]) for i in range(2)]
        ACCO = ps("ACCO", [128, 512])
        ACCS = ps("ACCS", [128, 512])
        YPS = ps("YPS", [128, 1024])
        TPb = [ACCO[:].bitcast(BF16)[:, 0:512], YPS[:, 0:512].bitcast(BF16)[:, 0:512]]
        T2b = [ACCS[:].bitcast(BF16)[:, 0:256], YPS[:, 512:1024].bitcast(BF16)[:, 0:256]]

        def ld(eng, dst, src, sig, chan, waits=()):
            A(eng, lambda e, d=dst, s=src: e.dma_start(out=d, in_=s), waits=waits, sig=sig, chan=chan, dma=True)

        consts = [
            (cT_sb[:], cT), (normw_sb[:], normwT), (lam_sb[:], lam_in), (subw[:], sublnT),
            (WN[:], wn_in), (fnw[:], fnw_in), (ident_bf[:], ident_bf_in), (ident_f[:], ident_f_in),
            (dtile[:], dtile_in), (sel[:], sel_in), (KA[0][64:68, :], kaug_in), (KA[1][64:68, :], kaug_in),
        ]
        for i, (d_, s_) in enumerate(consts):
            ld("sync", d_, s_, f"const{i}", "d_const")
        CONST_ALL = f"const{len(consts) - 1}"

        A("vector", lambda e: e.memset(ones_bf[:], 1.0))
        A("vector", lambda e: e.memset(ss[:], 0.0))
        A("vector", lambda e: e.memset(s64[:], 0.0))
        A("gpsimd", lambda e: e.memset(neghalf[:], -0.5))
        A("vector", lambda e: e.memset(ones_f[:], 1.0), sig="ones")
        A("scalar", lambda e: e.activation(out=sc[:], in_=cT_sb[:], func=ACTF.Exp, scale=-1.0),
          waits=[CONST_ALL], sig="p0_e")
        A("vector", lambda e: e.tensor_scalar(out=sc[:], in0=sc[:], scalar1=1.0, scalar2=None, op0=ALU.add),
          waits=["p0_e"], sig="p0_a")
        A("vector", lambda e: e.reciprocal(out=sc[:], in_=sc[:]), waits=["p0_a"], sig="p0_b")
        A("vector", lambda e: e.tensor_tensor(out=sc[:], in0=sc[:], in1=cT_sb[:], op=ALU.mult),
          waits=["p0_b", CONST_ALL], sig="p0_c")
        for kc in range(8):
            A("vector", lambda e, kc=kc: e.tensor_scalar(out=scb[:, kc, :], in0=ones_f[:], scalar1=sc[:, kc:kc + 1],
                                                         scalar2=None, op0=ALU.mult),
              waits=["p0_c", "ones"], sig=f"scb{kc}")
        A("vector", lambda e: e.tensor_tensor(out=lamp[:, 0:64], in0=lam_sb[:, 0:64], in1=lam_sb[:, 64:128], op=ALU.mult),
          waits=[CONST_ALL])
        A("vector", lambda e: e.tensor_tensor(out=lamp[:, 64:128], in0=lam_sb[:, 128:192], in1=lam_sb[:, 192:256], op=ALU.mult),
          sig="lam_p")
        A("vector", lambda e: e.tensor_reduce(out=small[:, 0:2], in_=lamp[:].rearrange("p (a b) -> p a b", a=2),
                                              axis=AX.X, op=ALU.add), waits=["lam_p"], sig="lam_s")
        A("scalar", lambda e: e.activation(out=small[:, 2:4], in_=small[:, 0:2], func=ACTF.Exp), waits=["lam_s"], sig="lam_e")
        A("vector", lambda e: e.tensor_tensor(out=nlam[:], in0=small[:, 3:4], in1=small[:, 2:3], op=ALU.subtract),
          waits=["lam_e"], sig="lam_d")
        A("vector", lambda e: e.tensor_scalar(out=nlam[:], in0=nlam[:], scalar1=-LAM_INIT, scalar2=None, op0=ALU.add),
          waits=["lam_d"], sig="nlam")
        A("vector", lambda e: e.tensor_scalar(out=subw[:], in0=subw[:], scalar1=1.0 - LAM_INIT, scalar2=None, op0=ALU.mult),
          waits=[CONST_ALL], sig="subw")

        stage_i = [0]
        stage_free = {}

        def stage_load(src_ap, width):
            i = stage_i[0]
            stage_i[0] += 1
            slot = i % 2
            sig = f"stg{i}"
            ld("sync" if slot == 0 else "gpsimd", wst[slot][:, 0:width], src_ap, sig, f"d_wst{slot}", waits=[stage_free.get(slot)])
            return slot, sig

        wav = w_ada.rearrange("(kc p) n -> kc p n", p=128)
        for cg in range(3):
            A("sync", lambda e, cg=cg: e.dma_start(out=badat[:], in_=bada[:, cg * 1024:(cg + 1) * 1024]),
              waits=[f"mod_ev{cg - 1}" if cg > 0 else None], sig=f"bada{cg}", chan="d_bada", dma=True)
            for kc in range(8):
                slot, sig = stage_load(wav[kc, :, cg * 1024:(cg + 1) * 1024], 1024)
                for half in range(2):
                    last = half == 1
                    A("tensor", lambda e, kc=kc, half=half, slot=slot: e.matmul(
                        YPS[:, half * 512:(half + 1) * 512], lhsT=scb[:, kc, :],
                        rhs=wst[slot][:, half * 512:(half + 1) * 512], start=(kc == 0), stop=(kc == 7)),
                      waits=[sig, f"scb{kc}", (f"mod_ev{cg - 1}" if (cg > 0 and kc == 0) else None)],
                      sig=(f"modmm{cg}_{kc}" if last else None), chan="pe_p0")
                stage_free[slot] = f"modmm{cg}_{kc}"
            dst = gate_bc if cg == 2 else mtmp
            A("vector", lambda e, dst=dst: e.tensor_tensor(out=dst[:], in0=YPS[:], in1=badat[:], op=ALU.add),
              waits=[f"modmm{cg}_7", f"bada{cg}", (f"mod_red{cg - 1}" if cg > 0 else None)], sig=f"mod_ev{cg}")
            if cg < 2:
                for kc in range(8):
                    A("vector", lambda e, kc=kc: e.tensor_tensor(out=prod[:, kc, :], in0=mtmp[:, kc * 128:(kc + 1) * 128],
                                                                 in1=ident_f[:], op=ALU.mult),
                      waits=[f"mod_ev{cg}", CONST_ALL], sig=f"mod_pr{cg}_{kc}")
                dstT = shiftT if cg == 0 else scaleT
                A("vector", lambda e, dstT=dstT: e.tensor_reduce(out=dstT[:], in_=prod[:], axis=AX.X, op=ALU.add),
                  waits=[f"mod_pr{cg}_7"], sig=f"mod_red{cg}")
        A("vector", lambda e: e.tensor_scalar(out=aT[:], in0=scaleT[:], scalar1=1.0, scalar2=None, op0=ALU.add),
          waits=["mod_red1"], sig="aT0")
        A("vector", lambda e: e.tensor_tensor(out=aT[:], in0=aT[:], in1=normw_sb[:], op=ALU.mult),
          waits=["aT0", CONST_ALL], sig="aT")

        sch.mark("p0a")
        win_v = w_in.rearrange("(kc p) n -> kc p n", p=128)
        for kc in range(8):
            slot, sig = stage_load(win_v[kc], 896)
            A("gpsimd", lambda e, kc=kc, slot=slot: e.tensor_copy(out=Win[:, kc, :], in_=wst[slot][:, 0:896]),
              waits=[sig], sig=f"win{kc}", chan="pool_w")
            stage_free[slot] = f"win{kc}"
        wout_v = w_out.rearrange("(kc p) n -> kc p n", p=128)
        for kc in range(8):
            slot, sig = stage_load(wout_v[kc], 1024)
            A("gpsimd", lambda e, kc=kc, slot=slot: e.tensor_copy(out=Wout[:, kc, :], in_=wst[slot][:, 0:1024]),
              waits=[sig], sig=f"wout{kc}", chan="pool_w")
            stage_free[slot] = f"wout{kc}"

        sch.mark("p0")
        xv = x.rearrange("(t p) d -> t p d", p=128)
        ldx = {}

        def issue_xload(t):
            if t >= NT:
                return
            ld("scalar", xt[t % 2], xv[t], f"xld{t}", f"d_xt{t % 2}", waits=[f"xn{t - 2}" if t >= 2 else None])

        fm_i = [0]
        spill_i = [0]
        spill_free = {}
        gt_i = [0]

        def spill(src_fn_eng, src_emit, rows, dram_ap, waits, tag):
            i = spill_i[0]
            spill_i[0] += 1
            slot = i % 3
            A(src_fn_eng, lambda e, slot=slot: src_emit(e, qst[slot][0:rows, :]),
              waits=list(waits) + [spill_free.get(slot)], sig=f"spw{i}", chan=f"c_{src_fn_eng}")
            A("sync", lambda e, slot=slot: e.dma_start(out=dram_ap, in_=qst[slot][0:rows, :]),
              waits=[f"spw{i}"], sig=tag, chan=f"d_sp{slot}", dma=True)
            spill_free[slot] = tag

        tq2 = [tq, sb("tq_b", [128, 192])]

        def xload(t):
            if t >= NT:
                return
            w = [f"xn{t - 3}"] if t >= 3 else []
            ld("scalar", xt[t % 3][:, 0:512], xv[t][:, 0:512], f"xlda{t}", f"d_xta{t % 3}", waits=w)
            ld("sync", xt[t % 3][:, 512:1024], xv[t][:, 512:1024], f"xldb{t}", f"d_xtb{t % 3}", waits=w)

        def norm_stage(B):
            def xn_op(t):
                A("vector", lambda e, t=t: e.tensor_scalar(out=xn[t % 8][:], in0=xt[t % 3], scalar1=rstd[:, t:t + 1],
                                                           scalar2=None, op0=ALU.mult),
                  waits=[f"rs{t}", f"sq{t}", f"xlda{t}", f"xldb{t}"], sig=f"xn{t}")
                xload(t + 3)
            for i in range(4):
                t = 4 * B + i
                A("scalar", lambda e, t=t: e.activation(out=xn[t % 8], in_=xt[t % 3], func=ACTF.Square,
                                                        accum_out=ss[:, t:t + 1]),
                  waits=[f"xlda{t}", f"xldb{t}", (f"tpd{B - 2}" if B >= 2 else None)], sig=f"sq{t}")
                A("vector", lambda e, t=t: e.tensor_scalar(out=rstd[:, t:t + 1], in0=ss[:, t:t + 1], scalar1=1.0 / D,
                                                           scalar2=EPS, op0=ALU.mult, op1=ALU.add),
                  waits=[f"sq{t}"], sig=f"rs0_{t}")
                A("gpsimd", lambda e, t=t: e.tensor_tensor(out=rstd[:, t:t + 1], in0=rstd[:, t:t + 1], in1=NH(rstd[:, t:t + 1]), op=ALU.pow),
                  waits=[f"rs0_{t}"], sig=f"rs{t}")
                if i >= 1:
                    xn_op(t - 1)
            xn_op(4 * B + 3)

        def tp_stage(B):
            hb = hT[B % 2]
            for fc in range(8):
                g = B * 8 + fc
                tpb = TPb[g % 2]
                for i in range(4):
                    t = 4 * B + i
                    A("tensor", lambda e, t=t, fc=fc, i=i, tpb=tpb: e.transpose(
                        tpb[:, i * 128:(i + 1) * 128], xn[t % 8][:, fc * 128:(fc + 1) * 128], ident_bf[:]),
                      waits=[f"xn{t}", CONST_ALL, "mod_ev2", (f"ht{g - 2}" if g >= 2 else None)],
                      sig=(f"tp{g}" if i == 3 else None), chan="pe_tp")
                A("vector", lambda e, fc=fc, hb=hb, tpb=tpb: e.tensor_scalar(
                    out=hb[:, fc, :], in0=tpb, scalar1=aT[:, fc:fc + 1], scalar2=shiftT[:, fc:fc + 1],
                    op0=ALU.mult, op1=ALU.add),
                  waits=[f"tp{g}", "aT", "mod_red0", (f"proj{B - 2}" if B >= 2 else None)], sig=f"ht{g}")
            sch.ev[f"tpd{B}"] = sch.ev[f"tp{B * 8 + 7}"]

        def fm_stage(B):
            hb = hT[B % 2]
            HT = f"ht{B * 8 + 7}"
            cs = slice(B * 512, (B + 1) * 512)
            fm_specs = [("qa0", 0, 64), ("qa1", 64, 64), ("ka0", 128, 64), ("ka1", 192, 64), ("ga", 256, 128), ("gb", 384, 128)]
            for name, c0, M in fm_specs:
                k = fm_i[0]
                fm_i[0] += 1
                fps = SP[0][0:M, (k % 2) * 512:(k % 2) * 512 + 512]
                for kc in range(8):
                    A("tensor", lambda e, kc=kc, c0=c0, M=M, fps=fps, hb=hb: e.matmul(
                        fps, lhsT=Win[:, kc, c0:c0 + M], rhs=hb[:, kc, :], start=(kc == 0), stop=(kc == 7)),
                      waits=[HT, "win7", (f"fme{k - 2}" if (k >= 2 and kc == 0) else None)],
                      sig=(f"fm{k}" if kc == 7 else None), chan="pe_fm")
                if name.startswith("qa"):
                    m = int(name[2])
                    spill("scalar", lambda e, dst, fps=fps: e.activation(out=dst, in_=fps, func=ACTF.Copy),
                          64, qa_scr[m][:, cs], [f"fm{k}"], f"qasp{m}_{B}")
                    sch.ev[f"fme{k}"] = sch.ev[f"spw{spill_i[0] - 1}"]
                elif name.startswith("ka"):
                    m = int(name[2])
                    A("scalar", lambda e, m=m, fps=fps, cs=cs: e.activation(out=KA[m][0:64, cs], in_=fps, func=ACTF.Copy),
                      waits=[f"fm{k}"], sig=f"fme{k}")
                else:
                    scr = sga_scr if name == "ga" else sgb_scr
                    spill("scalar", lambda e, dst, fps=fps: e.activation(out=dst, in_=fps, func=ACTF.Silu),
                          128, scr[:, cs], [f"fm{k}"], f"{name}sp_{B}")
                    sch.ev[f"fme{k}"] = sch.ev[f"spw{spill_i[0] - 1}"]

        def tm_stage(B):
            hb = hT[B % 2]
            HT = f"ht{B * 8 + 7}"
            cs = slice(B * 512, (B + 1) * 512)

            def tm_mm(i):
                t = 4 * B + i
                tps = SP[1][:, (t % 2) * 512:(t % 2) * 512 + 384]
                for kc in range(8):
                    A("tensor", lambda e, kc=kc, i=i, tps=tps: e.matmul(
                        tps, lhsT=hb[:, kc, i * 128:(i + 1) * 128], rhs=Win[:, kc, 512:896],
                        start=(kc == 0), stop=(kc == 7)),
                      waits=[HT, "win7", (f"tme{t - 2}" if (t >= 2 and kc == 0) else None)],
                      sig=(f"tm{t}" if kc == 7 else None), chan="pe_tm")
                if i == 3:
                    sch.ev[f"proj{B}"] = sch.ev[f"tm{t}"]

            def tm_evac(i):
                t = 4 * B + i
                tps = SP[1][:, (t % 2) * 512:(t % 2) * 512 + 384]
                tqt = tq2[t % 2]
                A("sync", lambda e, t=t: e.dma_start(out=ropet[t % 3][:], in_=rope_in[t]),
                  waits=[f"rp{t - 3}" if t >= 3 else None], sig=f"ropeld{t}", chan=f"d_rope{t % 3}", dma=True)
                A("scalar", lambda e, tps=tps, tqt=tqt: e.activation(out=tqt[:], in_=tps[:, 0:192], func=ACTF.Copy),
                  waits=[f"tm{t}", (f"rp{t - 2}" if t >= 2 else None)], sig=f"tqe{t}")
                A("scalar", lambda e, t=t, tps=tps: e.activation(out=VB[:, t, :], in_=tps[:, 192:256], func=ACTF.Copy))
                A("scalar", lambda e, t=t, tps=tps: e.activation(out=VA[:, t, :], in_=tps[:, 256:384], func=ACTF.Copy),
                  sig=f"tme{t}")
                A("vector", lambda e, tqt=tqt: e.tensor_tensor(out=tsq[:], in0=tqt[:], in1=tqt[:], op=ALU.mult),
                  waits=[f"tqe{t}"], sig=f"r0_{t}")
                A("vector", lambda e: e.tensor_reduce(out=ss3[:], in_=tsq[:].rearrange("p (h d) -> p h d", h=3),
                                                      axis=AX.X, op=ALU.add), waits=[f"r0_{t}"], sig=f"r1_{t}")
                A("vector", lambda e: e.tensor_scalar(out=rstd3[:], in0=ss3[:], scalar1=1.0 / 64, scalar2=EPS,
                                                      op0=ALU.mult, op1=ALU.add), waits=[f"r1_{t}"], sig=f"r2_{t}")
                A("gpsimd", lambda e: e.tensor_tensor(out=rstd3[:], in0=rstd3[:], in1=NH(rstd3[:]), op=ALU.pow),
                  waits=[f"r2_{t}"], sig=f"r3_{t}")
                for h in range(3):
                    A("vector", lambda e, h=h, tqt=tqt: e.scalar_tensor_tensor(
                        out=ty[:, h * 64:(h + 1) * 64], in0=tqt[:, h * 64:(h + 1) * 64], scalar=rstd3[:, h:h + 1],
                        in1=WN[:, h * 64:(h + 1) * 64], op0=ALU.mult, op1=ALU.mult),
                      waits=[f"r3_{t}", CONST_ALL], sig=f"r4_{t}_{h}")
                rp = ropet[t % 3]
                for h in range(3):
                    yh = ty[:, h * 64:(h + 1) * 64].rearrange("p (a s d) -> p a s d", a=2, s=2)
                    Bh = tB[:, h * 64:(h + 1) * 64].rearrange("p (a s d) -> p a s d", a=2, s=2)
                    sn = rp[:, 64:128].rearrange("p (a s d) -> p a s d", a=2, s=2)
                    A("vector", lambda e, h=h, rp=rp: e.tensor_tensor(out=tA[:, h * 64:(h + 1) * 64], in0=ty[:, h * 64:(h + 1) * 64],
                                                                      in1=rp[:, 0:64], op=ALU.mult),
                      waits=[f"r4_{t}_2", f"ropeld{t}"])
                    A("vector", lambda e, yh=yh, Bh=Bh, sn=sn: e.tensor_tensor(out=Bh[:, :, 0, :], in0=yh[:, :, 1, :],
                                                                               in1=sn[:, :, 0, :], op=ALU.mult))
                    A("vector", lambda e, yh=yh, Bh=Bh, sn=sn: e.tensor_tensor(out=Bh[:, :, 1, :], in0=yh[:, :, 0, :],
                                                                               in1=sn[:, :, 1, :], op=ALU.mult),
                      sig=f"r5_{t}_{h}")
                yot = yo[t % 2]
                A("vector", lambda e, yot=yot: e.tensor_tensor(out=yot[:, 0:192], in0=tA[:], in1=tB[:], op=ALU.add),
                  waits=[f"r5_{t}_2", (f"t2_{t - 2}" if t >= 2 else None)])
                A("vector", lambda e, yot=yot: e.tensor_tensor(out=yot[:, 192:256], in0=tA[:, 128:192], in1=tB[:, 128:192], op=ALU.add),
                  sig=f"rp{t}")

            def t2_ops(i):
                t = 4 * B + i
                yot = yo[t % 2]
                t2q = T2b[t % 2][:, 0:128]
                t2k = T2b[t % 2][:, 128:256]
                A("tensor", lambda e, yot=yot, t2q=t2q: e.transpose(t2q, yot[:, 0:128], ident_bf[:]),
                  waits=[f"rp{t}", (f"t2e{t - 2}" if t >= 2 else None)], chan="pe_t2")
                A("tensor", lambda e, yot=yot, t2k=t2k: e.transpose(t2k, yot[:, 128:256], ident_bf[:]),
                  sig=f"t2_{t}", chan="pe_t2")
                A("scalar", lambda e, t=t, i=i, t2q=t2q, B=B: e.activation(
                    out=QBst[B % 2][:, i * 128:(i + 1) * 128], in_=t2q, func=ACTF.Copy),
                  waits=[f"t2_{t}", (f"qbsp_{B - 2}" if (B >= 2 and i == 0) else None)], sig=f"t2q{t}")
                A("scalar", lambda e, t=t, t2k=t2k: e.activation(out=KB[:, t * 128:(t + 1) * 128], in_=t2k, func=ACTF.Copy),
                  waits=[f"t2_{t}"], sig=f"t2e{t}")

            tm_mm(0)
            tm_evac(0)
            for i in range(4):
                if i + 1 < 4:
                    tm_mm(i + 1)
                    tm_evac(i + 1)
                t2_ops(i)
            A("sync", lambda e, B=B, cs=cs: e.dma_start(out=qb_scr[:, cs], in_=QBst[B % 2][:]),
              waits=[f"t2q{4 * B + 3}"], sig=f"qbsp_{B}", chan=f"d_qbsp{B % 2}", dma=True)

        for t_ in range(3):
            xload(t_)
        norm_stage(0)
        tp_stage(0)
        for B in range(NB):
            if B + 1 < NB:
                norm_stage(B + 1)
            fm_stage(B)
            if B + 1 < NB:
                tp_stage(B + 1)
            tm_stage(B)
            sch.mark(f"p1_{B}")

        P1_DONE_ACT = f"t2e{NT - 1}"
        P1B = [P1_DONE_ACT, f"rp{NT - 1}", f"t2_{NT - 1}", f"xn{NT - 1}", "wout7", f"qbsp_{NB - 2}", f"qbsp_{NB - 1}"] + list(spill_free.values())

        pid = None
        g_i = [0]
        pass_i = [0]

        def block_loads(B):
            p = B % 2
            cs = slice(B * 512, (B + 1) * 512)
            prev = f"blkdone{B - 2}" if B >= 2 else None
            for m in range(2):
                ld("sync", Qb[p][m][0:64, :], qa_scr[m][:, cs], f"lq{B}_{m}b", f"d_blk{p}", waits=[prev, f"qasp{m}_{B}"] + (P1B if B < 2 else []))
                ld("sync", Qb[p][m][64:68, :], qaugb_in[:, cs], f"lq{B}_{m}bb", f"d_blk{p}")
                ld("sync", Qa[p][m][0:64, :], qa_scr[m][:, cs], f"lq{B}_{m}a", f"d_blk{p}")
                ld("sync", Qa[p][m][64:68, :], qauga_in[:, cs], f"lq{B}_{m}aa", f"d_blk{p}")
            for h in range(2):
                ld("sync", QBb[p][h][0:64, :], qb_scr[h * 64:(h + 1) * 64, cs], f"lqb{B}_{h}", f"d_blk{p}", waits=[f"qbsp_{B}"])
                ld("sync", QBb[p][h][64:128, :], qb_scr[h * 64:(h + 1) * 64, cs], f"lqb{B}_{h}d", f"d_blk{p}")
                ld("sync", SGB[p][h][:], sgb_scr[h * 64:(h + 1) * 64, cs], f"lsgb{B}_{h}", f"d_blk{p}", waits=[f"gbsp_{B}"])
            ld("sync", SGA[p][:], sga_scr[:, cs], f"blkld{B}", f"d_blk{p}", waits=[f"gasp_{B}"])

        def qk_ops(B, mp, c, dst):
            p = B % 2
            ops = []
            kc = slice(c * 128, (c + 1) * 128)
            if mp < 2:
                m = mp
                if c < 4 * B:
                    ops.append(lambda e: e.matmul(dst, lhsT=KA[m][0:68, kc], rhs=Qb[p][m][:, :], start=True, stop=True))
                elif c > 4 * B + 3:
                    ops.append(lambda e: e.matmul(dst, lhsT=KA[m][0:68, kc], rhs=Qa[p][m][:, :], start=True, stop=True))
                else:
                    j = c - 4 * B
                    if j > 0:
                        ops.append(lambda e: e.matmul(dst[:, 0:128 * j], lhsT=KA[m][0:68, kc], rhs=Qa[p][m][:, 0:128 * j],
                                                      start=True, stop=True))
                    ops.append(lambda e: e.matmul(dst[:, 128 * j:128 * (j + 1)], lhsT=KA[m][0:64, kc],
                                                  rhs=Qb[p][m][0:64, 128 * j:128 * (j + 1)], start=True, stop=False))
                    ops.append(lambda e: e.matmul(dst[:, 128 * j:128 * (j + 1)], lhsT=ident_bf[:], rhs=dtile[:],
                                                  start=False, stop=True))
                    if j < 3:
                        ops.append(lambda e: e.matmul(dst[:, 128 * (j + 1):512], lhsT=KA[m][0:68, kc],
                                                      rhs=Qb[p][m][:, 128 * (j + 1):512], start=True, stop=True))
            else:
                h = mp - 2
                r0 = 64 * (c % 2)
                ops.append(lambda e: e.matmul(dst, lhsT=KB[r0:r0 + 64, kc], rhs=QBb[p][h][r0:r0 + 64, :], start=True, stop=True))
            return ops

        def p3_pe(qb, extra=()):
            p = qb % 2
            for half in range(2):
                for ch in range(8):
                    A("tensor", lambda e, half=half, ch=ch, p=p: e.matmul(
                        YPS[:, half * 512:(half + 1) * 512], lhsT=Gg[p][:, ch, :], rhs=Wout[:, ch, half * 512:(half + 1) * 512],
                        start=(ch == 0), stop=(ch == 7)),
                      waits=[f"gg{qb}", "wout7", (f"z0_{qb - 1}" if qb >= 1 else "mod_ev2")] + list(extra),
                      sig=(f"y{qb}" if (half == 1 and ch == 7) else None), chan="pe_p3")

        def p3_dve(qb):
            p = qb % 2
            A("vector", lambda e: e.tensor_tensor(out=zt[:], in0=YPS[:], in1=gate_bc[:], op=ALU.mult),
              waits=[f"y{qb}", "mod_ev2"], sig=f"z0_{qb}")
            A("vector", lambda e, p=p: e.tensor_tensor(out=zt[:], in0=zt[:], in1=xot[p][:], op=ALU.add),
              waits=[f"z0_{qb}", f"xo{qb}"], sig=f"z1_{qb}")
            A("vector", lambda e: e.tensor_tensor(out=ot[qb % 2], in0=zt, in1=zt, op=ALU.mult), waits=[f"z1_{qb}", (f"st{qb - 2}" if qb >= 2 else None)], sig=f"z2_{qb}")
            A("vector", lambda e: e.tensor_reduce(out=zss[:, 0:1], in_=ot[qb % 2], axis=AX.X, op=ALU.add), waits=[f"z2_{qb}"], sig=f"z3_{qb}")
            A("vector", lambda e: e.tensor_scalar(out=zss[:, 1:2], in0=zss[:, 0:1], scalar1=1.0 / D, scalar2=EPS,
                                                  op0=ALU.mult, op1=ALU.add), waits=[f"z3_{qb}"], sig=f"z4_{qb}")
            A("gpsimd", lambda e: e.tensor_tensor(out=zss[:, 1:2], in0=zss[:, 1:2], in1=NH(zss[:, 1:2]), op=ALU.pow),
              waits=[f"z4_{qb}"], sig=f"z5_{qb}")
            A("vector", lambda e, p=p: e.scalar_tensor_tensor(out=ot[p][:], in0=zt[:], scalar=zss[:, 1:2], in1=fnw[:],
                                                              op0=ALU.mult, op1=ALU.mult),
              waits=[f"z5_{qb}", CONST_ALL, (f"st{qb - 2}" if qb >= 2 else None)], sig=f"z6_{qb}")
            A("sync", lambda e, p=p: e.dma_start(out=out[qb * 128:(qb + 1) * 128, :], in_=ot[p][:]),
              waits=[f"z6_{qb}"], sig=f"st{qb}", chan=f"d_st{p}", dma=True)

        def p3_loads(qb):
            p = qb % 2

            def gather(e, qb=qb, p=p):
                j = sch.pidj
                src = exout[qb].ap().rearrange("(c f) (t q) -> t f c q", f=128, q=128)
                return e.dma_start(out=Gg[p][:], in_=src[bass.ds(j, 1)].rearrange("o f c q -> (o f) c q"))
            A("sync", gather, waits=[f"cc{qb}", (f"y{qb - 2}" if qb >= 2 else None)], sig=f"gg{qb}", chan=f"d_gg{p}", dma=True)
            ld("sync", xot[p][:], xo[qb * 128:(qb + 1) * 128, :], f"xo{qb}", f"d_xo{p}", waits=[f"z1_{qb - 2}" if qb >= 2 else None])

        block_loads(0)
        block_loads(1)
        deferred = []
        last_fin_read = ["mod_ev2"]
        last_z0 = [None]
        FIN = YPS[:, 0:512]
        for B in range(NB):
            p = B % 2
            for mp in range(4):
                ps_i = pass_i[0]
                pass_i[0] += 1
                isA = mp < 2
                M = 128 if isA else 64
                gl = []
                for cp in range(32):
                    g = g_i[0]
                    g_i[0] += 1
                    gl.append(g)
                for idx in range(33):
                    for dd in [d_ for d_ in deferred if d_[0] == idx]:
                        dd[1]()
                        deferred.remove(dd)
                    if idx < 32:
                        g = gl[idx]
                        spd = SP[g % 2]
                        first = True
                        for sub in range(2):
                            c = 2 * idx + sub
                            ops = qk_ops(B, mp, c, spd[:, sub * 512:(sub + 1) * 512])
                            for oi, fn in enumerate(ops):
                                lastop = (sub == 1 and oi == len(ops) - 1)
                                w = []
                                if first:
                                    w = [f"exp{g - 2}" if g >= 2 else None, f"blkld{B}", P1_DONE_ACT, CONST_ALL]
                                    first = False
                                A("tensor", fn, waits=w, sig=(f"qk{g}" if lastop else None), chan="pe_qk")
                        A("scalar", lambda e, g=g, spd=spd: e.activation(out=PT[g % 3][:], in_=spd[:], func=ACTF.Exp, scale=0.125),
                          waits=[f"qk{g}", (f"pv{g - 3}" if g >= 3 else None)], sig=f"exp{g}", chan="act_exp")
                    if idx >= 1:
                        g = gl[idx - 1]
                        pt = PT[g % 3]
                        for sub in range(2):
                            c = 2 * (idx - 1) + sub
                            st = (c == 0)
                            sp_ = (c == NT - 1)
                            Vl = VA[:, c, :] if isA else VB[:, c, :]
                            w = [f"exp{g}"] if sub == 0 else []
                            if st:
                                w.append(f"evac{ps_i - 1}" if ps_i >= 1 else None)
                            A("tensor", lambda e, Vl=Vl, pt=pt, sub=sub, st=st, sp_=sp_, M=M: e.matmul(
                                ACCO[0:M, :], lhsT=Vl, rhs=pt[:, sub * 512:(sub + 1) * 512], start=st, stop=sp_), waits=w)
                        for sub in range(2):
                            c = 2 * (idx - 1) + sub
                            st = (c < 2)
                            sp_ = (c >= NT - 2)
                            A("tensor", lambda e, pt=pt, sub=sub, st=st, sp_=sp_: e.matmul(
                                ACCS[32 * sub:32 * sub + 1, :], lhsT=ones_bf[:, 0:1], rhs=pt[:, sub * 512:(sub + 1) * 512],
                                start=st, stop=sp_, tile_position=(0, 32 * sub)),
                              sig=(f"pv{g}" if sub == 1 else None), chan="pe_pv")
                LASTPV = f"pv{gl[-1]}"
                A("vector", lambda e, M=M: e.tensor_copy(out=Ocp[0:M, :], in_=ACCO[0:M, :]), waits=[LASTPV])
                A("vector", lambda e: e.tensor_copy(out=s64[0:1, :], in_=ACCS[0:1, :]))
                A("vector", lambda e: e.tensor_copy(out=s64[32:33, :], in_=ACCS[32:33, :]), sig=f"evac{ps_i}")
                if mp == 0 and B >= 1:
                    p3_pe(B - 1, [last_fin_read[0]])
                    p3_dve(B - 1)
                z0dep = f"z0_{B - 1}" if (B >= 1 and mp == 0) else last_z0[0]
                if B >= 1 and mp == 0:
                    last_z0[0] = f"z0_{B - 1}"
                deferred.append((2, lambda M=M, ps_i=ps_i, z0dep=z0dep: A(
                    "tensor", lambda e: e.matmul(FIN[0:M, :], lhsT=sel[:, 0:M], rhs=s64[:, :], start=True, stop=True),
                    waits=[f"evac{ps_i}", CONST_ALL, z0dep], sig=f"sbc{ps_i}", chan="pe_fin")))
                A("vector", lambda e, M=M: e.reciprocal(out=fR[0:M, :], in_=FIN[0:M, :]), waits=[f"sbc{ps_i}"], sig=f"f0_{ps_i}")
                last_fin_read[0] = f"f0_{ps_i}"
                if mp == 0:
                    A("vector", lambda e: e.tensor_tensor(out=fT1[:], in0=Ocp[:], in1=fR[:], op=ALU.mult),
                      waits=[f"f0_{ps_i}"], sig=f"fin{ps_i}")
                elif mp == 1:
                    A("vector", lambda e: e.tensor_tensor(out=fT2[:], in0=Ocp[:], in1=fR[:], op=ALU.mult),
                      waits=[f"f0_{ps_i}"], sig=f"f1_{ps_i}")
                    A("vector", lambda e: e.scalar_tensor_tensor(out=fD[:], in0=fT2[:], scalar=nlam[:, 0:1], in1=fT1[:],
                                                                 op0=ALU.mult, op1=ALU.add),
                      waits=[f"f1_{ps_i}", "nlam"], sig=f"f2_{ps_i}")
                    A("vector", lambda e: e.tensor_tensor(out=fD2[:], in0=fD[:], in1=fD[:], op=ALU.mult),
                      waits=[f"f2_{ps_i}"], sig=f"f3_{ps_i}")
                    deferred.append((8, lambda ps_i=ps_i: A(
                        "tensor", lambda e: e.matmul(FIN[:], lhsT=ones_f[:], rhs=fD2[:], start=True, stop=True),
                        waits=[f"f3_{ps_i}", "ones"], sig=f"ssn{ps_i}", chan="pe_fin")))
                    A("vector", lambda e: e.tensor_scalar(out=fRS[:], in0=FIN[:], scalar1=1.0 / 128, scalar2=EPS,
                                                          op0=ALU.mult, op1=ALU.add), waits=[f"ssn{ps_i}"], sig=f"f4_{ps_i}")
                    last_fin_read[0] = f"f4_{ps_i}"
                    deferred.append((12, lambda ps_i=ps_i: A(
                        "scalar", lambda e: e.activation(out=fRS[:], in_=fRS[:], func=ACTF.Ln), waits=[f"f4_{ps_i}"], sig=f"f4b_{ps_i}")))
                    deferred.append((12, lambda ps_i=ps_i: A(
                        "scalar", lambda e: e.activation(out=fRS[:], in_=fRS[:], func=ACTF.Exp, scale=-0.5),
                        waits=[f"f4b_{ps_i}"], sig=f"fin{ps_i}")))
                    A("vector", lambda e: e.scalar_tensor_tensor(out=fD[:], in0=fD[:], scalar=subw[:, 0:1], in1=fRS[:],
                                                                 op0=ALU.mult, op1=ALU.mult),
                      waits=[f"fin{ps_i}", "subw"], sig=f"f5_{ps_i}")
                    A("vector", lambda e, p=p: e.tensor_tensor(out=GA[p][:], in0=fD[:], in1=SGA[p][:], op=ALU.mult),
                      waits=[f"f5_{ps_i}", f"blkld{B}", (f"exw{B - 2}" if B >= 2 else None)], sig=f"ga{B}")
                else:
                    h = mp - 2
                    A("vector", lambda e: e.tensor_tensor(out=fT1[0:64, :], in0=Ocp[0:64, :], in1=fR[0:64, :], op=ALU.mult),
                      waits=[f"f0_{ps_i}"], sig=f"fin{ps_i}")
                    A("vector", lambda e, p=p, h=h: e.tensor_tensor(out=GB[p][h][:], in0=fT1[0:64, :], in1=SGB[p][h][:], op=ALU.mult),
                      waits=[f"fin{ps_i}", f"blkld{B}", (f"exw{B - 2}" if B >= 2 else None)], sig=f"gb{B}_{h}")
                if B == NB - 1 and mp == 3:
                    for _, fn_ in deferred:
                        fn_()
                    deferred.clear()
            sch.ev[f"blkdone{B}"] = sch.ev[f"gb{B}_1"]
            exi = exin[B].ap()
            A("gpsimd", lambda e, p=p, exi=exi: e.dma_start(out=exi[0:128, :], in_=GA[p][:]), waits=[f"ga{B}"],
              sig=f"exw{B}_0", chan=f"d_exw{p}", dma=True)
            A("gpsimd", lambda e, p=p, exi=exi: e.dma_start(out=exi[128:192, :], in_=GB[p][0][:]), waits=[f"gb{B}_0"],
              sig=f"exw{B}_1", chan=f"d_exw{p}", dma=True)
            A("gpsimd", lambda e, p=p, exi=exi: e.dma_start(out=exi[192:256, :], in_=GB[p][1][:]), waits=[f"gb{B}_1"],
              sig=f"exw{B}", chan=f"d_exw{p}", dma=True)
            A("gpsimd", lambda e, B=B: e.collective_compute(
                "AllGather", ALU.bypass, replica_groups=[[0, 1, 2, 3], [4, 5, 6, 7]],
                ins=[exin[B].ap()], outs=[exout[B].ap()]), waits=[f"exw{B}"], sig=f"cc{B}", chan="cc")
            sch.mark(f"p2a_{B}")
            if B + 2 < NB:
                block_loads(B + 2)
            p3_loads(B)
            sch.mark(f"p2_{B}")
        p3_pe(NB - 1, [last_fin_read[0]])
        p3_dve(NB - 1)
        A("sync", lambda e: e.nop(), waits=[f"st{NB - 2}", f"st{NB - 1}"])

        sch.limit = trunc
        chans = sorted(sch.cnt.keys())
        sems = {ch: es.enter_context(nc.semaphore(ch)) for ch in chans}
        with nc.Block() as block:
            @block.sync
            def _(e):
                sch.emit("sync", e, sems)

            @block.scalar
            def _(e):
                sch.emit("scalar", e, sems)

            @block.gpsimd
            def _(e):
                sch.emit("gpsimd", e, sems)

            @block.vector
            def _(e):
                sch.emit("vector", e, sems)

            @block.tensor
            def _(e):
                sch.emit("tensor", e, sems)
    return nc


def _host_inputs(inputs):
    f32 = np.float32
    bf = ml_dtypes.bfloat16
    x = np.asarray(inputs["x"], f32)
    c = np.asarray(inputs["c"], f32)
    w_ada = np.ascontiguousarray(np.asarray(inputs["w_ada"], f32)[0])
    b_ada = np.asarray(inputs["b_ada"], f32)[0]
    norm_w = np.asarray(inputs["norm_w"], f32)[0]
    w_in = np.asarray(inputs["w_in"], f32)[0]
    w_out = np.asarray(inputs["w_out"], f32)[0]
    lq1 = np.asarray(inputs["lambda_q1"], f32)[0]
    lk1 = np.asarray(inputs["lambda_k1"], f32)[0]
    lq2 = np.asarray(inputs["lambda_q2"], f32)[0]
    lk2 = np.asarray(inputs["lambda_k2"], f32)[0]
    subln = np.asarray(inputs["subln_w"], f32)[0]
    qw = np.asarray(inputs["q_norm_w"], f32)[0]
    kw = np.asarray(inputs["k_norm_w"], f32)[0]
    fnw = np.asarray(inputs["final_norm_w"], f32)

    def bc(v):
        return np.ascontiguousarray(np.broadcast_to(v[None, :], (128, v.shape[0])))

    tok = np.arange(S)
    row = (tok // 64).astype(f32)
    col = (tok % 64).astype(f32)
    freqs = (f32(1.0) / (f32(10000.0) ** (np.arange(16, dtype=f32) * f32(2.0) / f32(32)))).astype(f32)
    ang_r = (row[:, None] * freqs[None, :]).astype(f32)
    ang_c = (col[:, None] * freqs[None, :]).astype(f32)
    cr, sr, cc, sn_c = np.cos(ang_r), np.sin(ang_r), np.cos(ang_c), np.sin(ang_c)
    cos64 = np.concatenate([cr, cr, cc, cc], axis=1)
    sin64 = np.concatenate([-sr, sr, -sn_c, sn_c], axis=1)
    rope = np.ascontiguousarray(np.concatenate([cos64, sin64], axis=1).astype(f32).reshape(NT, 128, 128))
    ident = np.eye(128, dtype=f32)
    selm = np.zeros((64, 128), f32)
    selm[0, :] = 1.0
    selm[32, :] = 1.0
    w_out_rows = np.concatenate([np.concatenate([np.arange(r * 128, (r + 1) * 128),
                                                 512 + np.arange(r * 128, (r + 1) * 128)]) for r in range(4)])
    w_out_p = np.ascontiguousarray(w_out[w_out_rows])
    shared = dict(
        w_ada=w_ada, bada=bc(b_ada), normwT=np.ascontiguousarray(norm_w.reshape(8, 128).T),
        w_out=w_out_p, lam_in=bc(np.concatenate([lq1, lk1, lq2, lk2])), sublnT=np.ascontiguousarray(subln[:, None]),
        wn_in=bc(np.concatenate([qw, qw, kw])), fnw_in=bc(fnw), ident_bf=ident.astype(bf), ident_f=ident, rope=rope, sel=selm,
    )
    maps = []
    pos = np.arange(S)
    a_, r_ = (pos // 128).astype(f32), (pos % 128).astype(f32)
    for cid in range(8):
        b, j = cid // 4, cid % 4
        g = j // 2
        cols = np.concatenate([
            j * 128 + np.arange(128),
            512 + j * 128 + np.arange(128),
            1536 + j * 128 + np.arange(128),
            2816 + 2 * j * 64 + np.arange(128),
            2048 + 2 * j * 64 + np.arange(128),
            2560 + g * 64 + np.arange(64),
            2688 + g * 64 + np.arange(64),
            1024 + j * 128 + np.arange(128),
        ])
        sig = f32(8.0 * 2.0 ** (-2.0 * (j + 1)))
        kaug = np.stack([sig * r_, sig * 128 * a_, np.ones(S, f32), np.ones(S, f32)])
        qaugb = np.stack([np.ones(S, f32), np.ones(S, f32), -sig * 128 * a_, -sig * r_])
        ii = np.arange(128, dtype=f32)
        dt_ = -sig * np.abs(ii[None, :] - ii[:, None])
        xb = np.ascontiguousarray(x[b])
        m = dict(shared)
        m.update(
            x=xb, xo=np.ascontiguousarray(xb.reshape(NB, 4, 128, D)[:, j].reshape(NB * 128, D)),
            cT=np.ascontiguousarray(c[b].reshape(8, 128).T), w_in=np.ascontiguousarray(w_in[:, cols]),
            kaug=kaug.astype(bf), qaugb=qaugb.astype(bf), qauga=(-qaugb).astype(bf), dtile=dt_.astype(bf),
        )
        maps.append(m)
    return maps


_NC = None


def kernel(**inputs):
    global _NC
    if _NC is None:
        _NC = build_program()
    maps = _host_inputs(inputs)
    res = run_bass_kernel_spmd(_NC, maps, core_ids=list(range(8)))
    outp = np.empty((2, NB, 4, 128, D), np.float32)
    for cid in range(8):
        b, j = cid // 4, cid % 4
        outp[b, :, j] = np.asarray(res.results[cid]["out"], np.float32).reshape(NB, 128, D)
    return outp.reshape(2, S, D)
```

```python
import math
import numpy as np
import ml_dtypes
import concourse.bass as bass
import concourse.mybir as mybir
from concourse.bass_utils import run_bass_kernel_spmd

F32 = mybir.dt.float32
BF16 = mybir.dt.bfloat16
ALU = mybir.AluOpType
ACTF = mybir.ActivationFunctionType
AX = mybir.AxisListType

S = 8192
D = 1024
NT = S // 128
NB = S // 512
EPS = 1e-6
LAM_INIT = 0.8 - 0.6 * math.exp(0.0)
ENGS = ["sync", "scalar", "gpsimd", "vector", "tensor"]


class Sched:
    def __init__(self):
        self.ops = {e: [] for e in ENGS}
        self.cnt = {}
        self.ev = {}
        self.chan_eng = {}
        self.marks = {}
        self.seq = []
        self.limit = None

    def mark(self, name):
        self.marks[name] = {e: len(self.ops[e]) for e in ENGS}

    def add(self, eng, fn, waits=(), sig=None, chan=None, dma=False):
        ch = None
        inc = 16 if dma else 1
        if sig is not None:
            ch = chan or ("c_" + eng)
            assert self.chan_eng.setdefault(ch, eng) == eng, (ch, eng)
            self.cnt[ch] = self.cnt.get(ch, 0) + inc
            assert sig not in self.ev, sig
            self.ev[sig] = (ch, self.cnt[ch])
        self.ops[eng].append((fn, tuple(w for w in waits if w is not None), sig, ch, inc))
        self.seq.append(eng)

    def emit(self, ename, eng, sems):
        waited = {}
        if ename == "sync":
            self.pidj = eng.partition_id() % 4
        ops = self.ops[ename]
        if self.limit is not None:
            if self.limit.startswith("n:"):
                ops = ops[:self.seq[:int(self.limit[2:])].count(ename)]
            else:
                ops = ops[:self.marks[self.limit][ename]]
        for fn, waits, sig, ch, inc in ops:
            for w in waits:
                wch, val = self.ev[w]
                if waited.get(wch, 0) >= val:
                    continue
                eng.wait_ge(sems[wch], val)
                waited[wch] = val
            ins = fn(eng)
            if sig is not None:
                ins.then_inc(sems[ch], inc)


def build_program(trunc=None):
    nc = bass.Bass("TRN2", target_bir_lowering=False)

    def din(name, shape, dt=F32):
        return nc.dram_tensor(name, list(shape), dt, kind="ExternalInput").ap()

    x = din("x", [S, D])
    xo = din("xo", [NB * 128, D])
    cT = din("cT", [128, 8])
    w_ada = din("w_ada", [D, 3 * D])
    bada = din("bada", [128, 3 * D])
    normwT = din("normwT", [128, 8])
    w_in = din("w_in", [D, 896])
    w_out = din("w_out", [D, D])
    lam_in = din("lam_in", [128, 256])
    sublnT = din("sublnT", [128, 1])
    wn_in = din("wn_in", [128, 192])
    fnw_in = din("fnw_in", [128, D])
    ident_bf_in = din("ident_bf", [128, 128], BF16)
    ident_f_in = din("ident_f", [128, 128])
    rope_in = din("rope", [NT, 128, 128])
    kaug_in = din("kaug", [4, S], BF16)
    qaugb_in = din("qaugb", [4, S], BF16)
    qauga_in = din("qauga", [4, S], BF16)
    dtile_in = din("dtile", [128, 128], BF16)
    sel_in = din("sel", [64, 128])
    out = nc.dram_tensor("out", [NB * 128, D], F32, kind="ExternalOutput").ap()

    qa_scr = [nc.dram_tensor(f"qa_scr{m}", [64, S], BF16).ap() for m in range(2)]
    qb_scr = nc.dram_tensor("qb_scr", [128, S], BF16).ap()
    sga_scr = nc.dram_tensor("sga_scr", [128, S], BF16).ap()
    sgb_scr = nc.dram_tensor("sgb_scr", [128, S], BF16).ap()
    exin = [nc.dram_tensor(f"exin{i}", [256, 512], BF16) for i in range(NB)]
    exout = [nc.dram_tensor(f"exout{i}", [1024, 512], BF16) for i in range(NB)]

    sch = Sched()
    A = sch.add
    from contextlib import ExitStack
    es = ExitStack()

    def sb(name, shape, dt=F32):
        return es.enter_context(nc.sbuf_tensor("s_" + name, list(shape), dt))

    def ps(name, shape, dt=F32):
        return es.enter_context(nc.psum_tensor("p_" + name, list(shape), dt))

    with es:
        KA = [sb(f"KA{m}", [68, S], BF16) for m in range(2)]
        KB = sb("KB", [128, S], BF16)
        VA = sb("VA", [128, NT, 128], BF16)
        VB = sb("VB", [128, NT, 64], BF16)
        Wout = sb("Wout", [128, 8, D], BF16)
        hT = [sb(f"hT{i}", [128, 8, 512], BF16) for i in range(2)]
        h0f = hT[0][:].rearrange("p a b -> p (a b)").bitcast(F32)
        h1f = hT[1][:].rearrange("p a b -> p (a b)").bitcast(F32)
        mtmp = h0f[:, 0:1024]
        prod = h0f[:, 1024:2048].rearrange("p (a b) -> p a b", a=8)
        badat = h1f[:, 0:1024]
        scb = h1f[:, 1024:2048].rearrange("p (a b) -> p a b", a=8)
        ropet = [sb(f"ropet{i}", [128, 128]) for i in range(3)]
        gate_bc = sb("gate_bc", [128, D])
        fnw = sb("fnw", [128, D])
        small = sb("small", [128, 64])
        sc = sb("sc", [128, 8])
        shiftT = sb("shiftT", [128, 8])
        scaleT = sb("scaleT", [128, 8])
        aT = sb("aT", [128, 8])
        normw_sb = sb("normw_sb", [128, 8])
        cT_sb = sb("cT_sb", [128, 8])
        lam_sb = sb("lam_sb", [128, 256])
        lamp = sb("lamp", [128, 128])
        nlam = sb("nlam", [128, 1])
        subw = sb("subw", [128, 1])
        WN = sb("WN", [128, 192])
        ident_bf = sb("ident_bf", [128, 128], BF16)
        ident_f = sb("ident_f", [128, 128])
        ones_bf = sb("ones_bf", [128, 128], BF16)
        ones_f = sb("ones_f", [128, 128])
        dtile = sb("dtile", [128, 128], BF16)
        sel = sb("sel", [64, 128])
        s64 = sb("s64", [64, 512])
        ss = sb("ss", [128, NT])
        rstd = sb("rstd", [128, NT])
        ss3 = sb("ss3", [128, 3])
        rstd3 = sb("rstd3", [128, 3])
        yo = [sb(f"yo{i}", [128, 256], BF16) for i in range(2)]
        UN = 32768
        U = sb("U", [128, UN], BF16)
        cur = [0]

        def cv(shape, dt=BF16, rows=128):
            n = 1
            for d_ in shape[1:]:
                n *= d_
            nb = n * (2 if dt == F32 else 1)
            o = cur[0]
            cur[0] += nb
            assert cur[0] <= UN, cur[0]
            v = U[0:shape[0], o:o + nb]
            if dt == F32:
                v = v.bitcast(F32)
            if len(shape) == 3:
                v = v.rearrange("p (a b) -> p a b", a=shape[1])
            return v
        xn = [cv([128, D]) for i in range(8)]
        xt = [cv([128, D], F32) for i in range(3)]
        wst = [cv([128, 1024], F32) for i in range(2)]
        Win = cv([128, 8, 896])
        qst = [cv([128, 512]) for i in range(3)]
        gtmp = [cv([128, 512], F32) for i in range(2)]
        QBst = [cv([128, 512]) for i in range(2)]
        tq = cv([128, 192], F32)
        tsq = cv([128, 192], F32)
        ty = cv([128, 192], F32)
        tA = cv([128, 192], F32)
        tB = cv([128, 192], F32)
        print("phase1 union bytes", cur[0] * 2)
        cur[0] = 0
        Qb = [[cv([68, 512]) for m in range(2)] for p in range(2)]
        Qa = [[cv([68, 512]) for m in range(2)] for p in range(2)]
        QBb = [[cv([128, 512]) for h in range(2)] for p in range(2)]
        SGA = [cv([128, 512]) for p in range(2)]
        SGB = [[cv([64, 512]) for h in range(2)] for p in range(2)]
        PT = [cv([128, 1024]) for i in range(3)]
        fR = cv([128, 512], F32)
        fT1 = cv([128, 512], F32)
        fT2 = cv([128, 512], F32)
        fD = cv([128, 512], F32)
        Ocp = cv([128, 512], F32)
        fD2 = fT2
        fRS = fR
        GA = [cv([128, 512]) for p in range(2)]
        GB = [[cv([64, 512]) for h in range(2)] for p in range(2)]
        Gg = [cv([128, 8, 128]) for p in range(2)]
        xot = [cv([128, D], F32) for p in range(2)]
        zt = cv([128, D], F32)
        ot = [cv([128, D], F32) for p in range(2)]
        zss = sb("zss", [128, 2])
        neghalf = sb("neghalf", [128, 512])

        def NH(ap):
            return neghalf[0:ap.shape[0], 0:ap.shape[1]]
        print("phase2 union bytes", cur[0] * 2)

        SP = [ps(f"SP{i}", [128, 1024]) for i in range(2)]
        ACCO = ps("ACCO", [128, 512])
        ACCS = ps("ACCS", [128, 512])
        YPS = ps("YPS", [128, 1024])
        TPb = [ACCO[:].bitcast(BF16)[:, 0:512], YPS[:, 0:512].bitcast(BF16)[:, 0:512]]
        T2b = [ACCS[:].bitcast(BF16)[:, 0:256], YPS[:, 512:1024].bitcast(BF16)[:, 0:256]]

        def ld(eng, dst, src, sig, chan, waits=()):
            A(eng, lambda e, d=dst, s=src: e.dma_start(out=d, in_=s), waits=waits, sig=sig, chan=chan, dma=True)

        consts = [
            (cT_sb[:], cT), (normw_sb[:], normwT), (lam_sb[:], lam_in), (subw[:], sublnT),
            (WN[:], wn_in), (fnw[:], fnw_in), (ident_bf[:], ident_bf_in), (ident_f[:], ident_f_in),
            (dtile[:], dtile_in), (sel[:], sel_in), (KA[0][64:68, :], kaug_in), (KA[1][64:68, :], kaug_in),
        ]
        for i, (d_, s_) in enumerate(consts):
            ld("sync", d_, s_, f"const{i}", "d_const")
        CONST_ALL = f"const{len(consts) - 1}"

        A("vector", lambda e: e.memset(ones_bf[:], 1.0))
        A("vector", lambda e: e.memset(ss[:], 0.0))
        A("vector", lambda e: e.memset(s64[:], 0.0))
        A("gpsimd", lambda e: e.memset(neghalf[:], -0.5))
        A("vector", lambda e: e.memset(ones_f[:], 1.0), sig="ones")
        A("scalar", lambda e: e.activation(out=sc[:], in_=cT_sb[:], func=ACTF.Exp, scale=-1.0),
          waits=[CONST_ALL], sig="p0_e")
        A("vector", lambda e: e.tensor_scalar(out=sc[:], in0=sc[:], scalar1=1.0, scalar2=None, op0=ALU.add),
          waits=["p0_e"], sig="p0_a")
        A("vector", lambda e: e.reciprocal(out=sc[:], in_=sc[:]), waits=["p0_a"], sig="p0_b")
        A("vector", lambda e: e.tensor_tensor(out=sc[:], in0=sc[:], in1=cT_sb[:], op=ALU.mult),
          waits=["p0_b", CONST_ALL], sig="p0_c")
        for kc in range(8):
            A("vector", lambda e, kc=kc: e.tensor_scalar(out=scb[:, kc, :], in0=ones_f[:], scalar1=sc[:, kc:kc + 1],
                                                         scalar2=None, op0=ALU.mult),
              waits=["p0_c", "ones"], sig=f"scb{kc}")
        A("vector", lambda e: e.tensor_tensor(out=lamp[:, 0:64], in0=lam_sb[:, 0:64], in1=lam_sb[:, 64:128], op=ALU.mult),
          waits=[CONST_ALL])
        A("vector", lambda e: e.tensor_tensor(out=lamp[:, 64:128], in0=lam_sb[:, 128:192], in1=lam_sb[:, 192:256], op=ALU.mult),
          sig="lam_p")
        A("vector", lambda e: e.tensor_reduce(out=small[:, 0:2], in_=lamp[:].rearrange("p (a b) -> p a b", a=2),
                                              axis=AX.X, op=ALU.add), waits=["lam_p"], sig="lam_s")
        A("scalar", lambda e: e.activation(out=small[:, 2:4], in_=small[:, 0:2], func=ACTF.Exp), waits=["lam_s"], sig="lam_e")
        A("vector", lambda e: e.tensor_tensor(out=nlam[:], in0=small[:, 3:4], in1=small[:, 2:3], op=ALU.subtract),
          waits=["lam_e"], sig="lam_d")
        A("vector", lambda e: e.tensor_scalar(out=nlam[:], in0=nlam[:], scalar1=-LAM_INIT, scalar2=None, op0=ALU.add),
          waits=["lam_d"], sig="nlam")
        A("vector", lambda e: e.tensor_scalar(out=subw[:], in0=subw[:], scalar1=1.0 - LAM_INIT, scalar2=None, op0=ALU.mult),
          waits=[CONST_ALL], sig="subw")

        stage_i = [0]
        stage_free = {}

        def stage_load(src_ap, width):
            i = stage_i[0]
            stage_i[0] += 1
            slot = i % 2
            sig = f"stg{i}"
            ld("sync", wst[slot][:, 0:width], src_ap, sig, f"d_wst{slot}", waits=[stage_free.get(slot)])
            return slot, sig

        wav = w_ada.rearrange("(kc p) n -> kc p n", p=128)
        for cg in range(3):
            A("sync", lambda e, cg=cg: e.dma_start(out=badat[:], in_=bada[:, cg * 1024:(cg + 1) * 1024]),
              waits=[f"mod_ev{cg - 1}" if cg > 0 else None], sig=f"bada{cg}", chan="d_bada", dma=True)
            for kc in range(8):
                slot, sig = stage_load(wav[kc, :, cg * 1024:(cg + 1) * 1024], 1024)
                for half in range(2):
                    last = half == 1
                    A("tensor", lambda e, kc=kc, half=half, slot=slot: e.matmul(
                        YPS[:, half * 512:(half + 1) * 512], lhsT=scb[:, kc, :],
                        rhs=wst[slot][:, half * 512:(half + 1) * 512], start=(kc == 0), stop=(kc == 7)),
                      waits=[sig, f"scb{kc}", (f"mod_ev{cg - 1}" if (cg > 0 and kc == 0) else None)],
                      sig=(f"modmm{cg}_{kc}" if last else None), chan="pe_p0")
                stage_free[slot] = f"modmm{cg}_{kc}"
            dst = gate_bc if cg == 2 else mtmp
            A("vector", lambda e, dst=dst: e.tensor_tensor(out=dst[:], in0=YPS[:], in1=badat[:], op=ALU.add),
              waits=[f"modmm{cg}_7", f"bada{cg}", (f"mod_red{cg - 1}" if cg > 0 else None)], sig=f"mod_ev{cg}")
            if cg < 2:
                for kc in range(8):
                    A("vector", lambda e, kc=kc: e.tensor_tensor(out=prod[:, kc, :], in0=mtmp[:, kc * 128:(kc + 1) * 128],
                                                                 in1=ident_f[:], op=ALU.mult),
                      waits=[f"mod_ev{cg}", CONST_ALL], sig=f"mod_pr{cg}_{kc}")
                dstT = shiftT if cg == 0 else scaleT
                A("vector", lambda e, dstT=dstT: e.tensor_reduce(out=dstT[:], in_=prod[:], axis=AX.X, op=ALU.add),
                  waits=[f"mod_pr{cg}_7"], sig=f"mod_red{cg}")
        A("vector", lambda e: e.tensor_scalar(out=aT[:], in0=scaleT[:], scalar1=1.0, scalar2=None, op0=ALU.add),
          waits=["mod_red1"], sig="aT0")
        A("vector", lambda e: e.tensor_tensor(out=aT[:], in0=aT[:], in1=normw_sb[:], op=ALU.mult),
          waits=["aT0", CONST_ALL], sig="aT")

        sch.mark("p0a")
        win_v = w_in.rearrange("(kc p) n -> kc p n", p=128)
        for kc in range(8):
            slot, sig = stage_load(win_v[kc], 896)
            A("gpsimd", lambda e, kc=kc, slot=slot: e.tensor_copy(out=Win[:, kc, :], in_=wst[slot][:, 0:896]),
              waits=[sig], sig=f"win{kc}", chan="pool_w")
            stage_free[slot] = f"win{kc}"
        wout_v = w_out.rearrange("(kc p) n -> kc p n", p=128)
        for kc in range(8):
            slot, sig = stage_load(wout_v[kc], 1024)
            A("gpsimd", lambda e, kc=kc, slot=slot: e.tensor_copy(out=Wout[:, kc, :], in_=wst[slot][:, 0:1024]),
              waits=[sig], sig=f"wout{kc}", chan="pool_w")
            stage_free[slot] = f"wout{kc}"

        sch.mark("p0")
        xv = x.rearrange("(t p) d -> t p d", p=128)
        ldx = {}

        def issue_xload(t):
            if t >= NT:
                return
            ld("scalar", xt[t % 2], xv[t], f"xld{t}", f"d_xt{t % 2}", waits=[f"xn{t - 2}" if t >= 2 else None])

        fm_i = [0]
        spill_i = [0]
        spill_free = {}
        gt_i = [0]

        def spill(src_fn_eng, src_emit, rows, dram_ap, waits, tag):
            i = spill_i[0]
            spill_i[0] += 1
            slot = i % 3
            A(src_fn_eng, lambda e, slot=slot: src_emit(e, qst[slot][0:rows, :]),
              waits=list(waits) + [spill_free.get(slot)], sig=f"spw{i}", chan=f"c_{src_fn_eng}")
            A("sync", lambda e, slot=slot: e.dma_start(out=dram_ap, in_=qst[slot][0:rows, :]),
              waits=[f"spw{i}"], sig=tag, chan=f"d_sp{slot}", dma=True)
            spill_free[slot] = tag

        tq2 = [tq, sb("tq_b", [128, 192])]

        def xload(t):
            if t >= NT:
                return
            w = [f"xn{t - 3}"] if t >= 3 else []
            ld("scalar", xt[t % 3][:, 0:512], xv[t][:, 0:512], f"xlda{t}", f"d_xta{t % 3}", waits=w)
            ld("sync", xt[t % 3][:, 512:1024], xv[t][:, 512:1024], f"xldb{t}", f"d_xtb{t % 3}", waits=w)

        def norm_stage(B):
            def xn_op(t):
                A("vector", lambda e, t=t: e.tensor_scalar(out=xn[t % 8][:], in0=xt[t % 3], scalar1=rstd[:, t:t + 1],
                                                           scalar2=None, op0=ALU.mult),
                  waits=[f"rs{t}", f"sq{t}", f"xlda{t}", f"xldb{t}"], sig=f"xn{t}")
                xload(t + 3)
            for i in range(4):
                t = 4 * B + i
                A("scalar", lambda e, t=t: e.activation(out=xn[t % 8], in_=xt[t % 3], func=ACTF.Square,
                                                        accum_out=ss[:, t:t + 1]),
                  waits=[f"xlda{t}", f"xldb{t}", (f"tpd{B - 2}" if B >= 2 else None)], sig=f"sq{t}")
                A("vector", lambda e, t=t: e.tensor_scalar(out=rstd[:, t:t + 1], in0=ss[:, t:t + 1], scalar1=1.0 / D,
                                                           scalar2=EPS, op0=ALU.mult, op1=ALU.add),
                  waits=[f"sq{t}"], sig=f"rs0_{t}")
                A("gpsimd", lambda e, t=t: e.tensor_tensor(out=rstd[:, t:t + 1], in0=rstd[:, t:t + 1], in1=NH(rstd[:, t:t + 1]), op=ALU.pow),
                  waits=[f"rs0_{t}"], sig=f"rs{t}")
                if i >= 1:
                    xn_op(t - 1)
            xn_op(4 * B + 3)

        def tp_stage(B):
            hb = hT[B % 2]
            for fc in range(8):
                g = B * 8 + fc
                tpb = TPb[g % 2]
                for i in range(4):
                    t = 4 * B + i
                    A("tensor", lambda e, t=t, fc=fc, i=i, tpb=tpb: e.transpose(
                        tpb[:, i * 128:(i + 1) * 128], xn[t % 8][:, fc * 128:(fc + 1) * 128], ident_bf[:]),
                      waits=[f"xn{t}", CONST_ALL, "mod_ev2", (f"ht{g - 2}" if g >= 2 else None)],
                      sig=(f"tp{g}" if i == 3 else None), chan="pe_tp")
                A("vector", lambda e, fc=fc, hb=hb, tpb=tpb: e.tensor_scalar(
                    out=hb[:, fc, :], in0=tpb, scalar1=aT[:, fc:fc + 1], scalar2=shiftT[:, fc:fc + 1],
                    op0=ALU.mult, op1=ALU.add),
                  waits=[f"tp{g}", "aT", "mod_red0", (f"proj{B - 2}" if B >= 2 else None)], sig=f"ht{g}")
            sch.ev[f"tpd{B}"] = sch.ev[f"tp{B * 8 + 7}"]

        def fm_stage(B):
            hb = hT[B % 2]
            HT = f"ht{B * 8 + 7}"
            cs = slice(B * 512, (B + 1) * 512)
            fm_specs = [("qa0", 0, 64), ("qa1", 64, 64), ("ka0", 128, 64), ("ka1", 192, 64), ("ga", 256, 128), ("gb", 384, 128)]
            for name, c0, M in fm_specs:
                k = fm_i[0]
                fm_i[0] += 1
                fps = SP[0][0:M, (k % 2) * 512:(k % 2) * 512 + 512]
                for kc in range(8):
                    A("tensor", lambda e, kc=kc, c0=c0, M=M, fps=fps, hb=hb: e.matmul(
                        fps, lhsT=Win[:, kc, c0:c0 + M], rhs=hb[:, kc, :], start=(kc == 0), stop=(kc == 7)),
                      waits=[HT, "win7", (f"fme{k - 2}" if (k >= 2 and kc == 0) else None)],
                      sig=(f"fm{k}" if kc == 7 else None), chan="pe_fm")
                if name.startswith("qa"):
                    m = int(name[2])
                    spill("scalar", lambda e, dst, fps=fps: e.activation(out=dst, in_=fps, func=ACTF.Copy),
                          64, qa_scr[m][:, cs], [f"fm{k}"], f"qasp{m}_{B}")
                    sch.ev[f"fme{k}"] = sch.ev[f"spw{spill_i[0] - 1}"]
                elif name.startswith("ka"):
                    m = int(name[2])
                    A("scalar", lambda e, m=m, fps=fps, cs=cs: e.activation(out=KA[m][0:64, cs], in_=fps, func=ACTF.Copy),
                      waits=[f"fm{k}"], sig=f"fme{k}")
                else:
                    scr = sga_scr if name == "ga" else sgb_scr
                    spill("scalar", lambda e, dst, fps=fps: e.activation(out=dst, in_=fps, func=ACTF.Silu),
                          128, scr[:, cs], [f"fm{k}"], f"{name}sp_{B}")
                    sch.ev[f"fme{k}"] = sch.ev[f"spw{spill_i[0] - 1}"]

        def tm_stage(B):
            hb = hT[B % 2]
            HT = f"ht{B * 8 + 7}"
            cs = slice(B * 512, (B + 1) * 512)

            def tm_mm(i):
                t = 4 * B + i
                tps = SP[1][:, (t % 2) * 512:(t % 2) * 512 + 384]
                for kc in range(8):
                    A("tensor", lambda e, kc=kc, i=i, tps=tps: e.matmul(
                        tps, lhsT=hb[:, kc, i * 128:(i + 1) * 128], rhs=Win[:, kc, 512:896],
                        start=(kc == 0), stop=(kc == 7)),
                      waits=[HT, "win7", (f"tme{t - 2}" if (t >= 2 and kc == 0) else None)],
                      sig=(f"tm{t}" if kc == 7 else None), chan="pe_tm")
                if i == 3:
                    sch.ev[f"proj{B}"] = sch.ev[f"tm{t}"]

            def tm_evac(i):
                t = 4 * B + i
                tps = SP[1][:, (t % 2) * 512:(t % 2) * 512 + 384]
                tqt = tq2[t % 2]
                A("sync", lambda e, t=t: e.dma_start(out=ropet[t % 3][:], in_=rope_in[t]),
                  waits=[f"rp{t - 3}" if t >= 3 else None], sig=f"ropeld{t}", chan=f"d_rope{t % 3}", dma=True)
                A("scalar", lambda e, tps=tps, tqt=tqt: e.activation(out=tqt[:], in_=tps[:, 0:192], func=ACTF.Copy),
                  waits=[f"tm{t}", (f"rp{t - 2}" if t >= 2 else None)], sig=f"tqe{t}")
                A("scalar", lambda e, t=t, tps=tps: e.activation(out=VB[:, t, :], in_=tps[:, 192:256], func=ACTF.Copy))
                A("scalar", lambda e, t=t, tps=tps: e.activation(out=VA[:, t, :], in_=tps[:, 256:384], func=ACTF.Copy),
                  sig=f"tme{t}")
                A("vector", lambda e, tqt=tqt: e.tensor_tensor(out=tsq[:], in0=tqt[:], in1=tqt[:], op=ALU.mult),
                  waits=[f"tqe{t}"], sig=f"r0_{t}")
                A("vector", lambda e: e.tensor_reduce(out=ss3[:], in_=tsq[:].rearrange("p (h d) -> p h d", h=3),
                                                      axis=AX.X, op=ALU.add), waits=[f"r0_{t}"], sig=f"r1_{t}")
                A("vector", lambda e: e.tensor_scalar(out=rstd3[:], in0=ss3[:], scalar1=1.0 / 64, scalar2=EPS,
                                                      op0=ALU.mult, op1=ALU.add), waits=[f"r1_{t}"], sig=f"r2_{t}")
                A("gpsimd", lambda e: e.tensor_tensor(out=rstd3[:], in0=rstd3[:], in1=NH(rstd3[:]), op=ALU.pow),
                  waits=[f"r2_{t}"], sig=f"r3_{t}")
                for h in range(3):
                    A("vector", lambda e, h=h, tqt=tqt: e.scalar_tensor_tensor(
                        out=ty[:, h * 64:(h + 1) * 64], in0=tqt[:, h * 64:(h + 1) * 64], scalar=rstd3[:, h:h + 1],
                        in1=WN[:, h * 64:(h + 1) * 64], op0=ALU.mult, op1=ALU.mult),
                      waits=[f"r3_{t}", CONST_ALL], sig=f"r4_{t}_{h}")
                rp = ropet[t % 3]
                for h in range(3):
                    yh = ty[:, h * 64:(h + 1) * 64].rearrange("p (a s d) -> p a s d", a=2, s=2)
                    Bh = tB[:, h * 64:(h + 1) * 64].rearrange("p (a s d) -> p a s d", a=2, s=2)
                    sn = rp[:, 64:128].rearrange("p (a s d) -> p a s d", a=2, s=2)
                    A("vector", lambda e, h=h, rp=rp: e.tensor_tensor(out=tA[:, h * 64:(h + 1) * 64], in0=ty[:, h * 64:(h + 1) * 64],
                                                                      in1=rp[:, 0:64], op=ALU.mult),
                      waits=[f"r4_{t}_2", f"ropeld{t}"])
                    A("vector", lambda e, yh=yh, Bh=Bh, sn=sn: e.tensor_tensor(out=Bh[:, :, 0, :], in0=yh[:, :, 1, :],
                                                                               in1=sn[:, :, 0, :], op=ALU.mult))
                    A("vector", lambda e, yh=yh, Bh=Bh, sn=sn: e.tensor_tensor(out=Bh[:, :, 1, :], in0=yh[:, :, 0, :],
                                                                               in1=sn[:, :, 1, :], op=ALU.mult),
                      sig=f"r5_{t}_{h}")
                yot = yo[t % 2]
                A("vector", lambda e, yot=yot: e.tensor_tensor(out=yot[:, 0:192], in0=tA[:], in1=tB[:], op=ALU.add),
                  waits=[f"r5_{t}_2", (f"t2_{t - 2}" if t >= 2 else None)])
                A("vector", lambda e, yot=yot: e.tensor_tensor(out=yot[:, 192:256], in0=tA[:, 128:192], in1=tB[:, 128:192], op=ALU.add),
                  sig=f"rp{t}")

            def t2_ops(i):
                t = 4 * B + i
                yot = yo[t % 2]
                t2q = T2b[t % 2][:, 0:128]
                t2k = T2b[t % 2][:, 128:256]
                A("tensor", lambda e, yot=yot, t2q=t2q: e.transpose(t2q, yot[:, 0:128], ident_bf[:]),
                  waits=[f"rp{t}", (f"t2e{t - 2}" if t >= 2 else None)], chan="pe_t2")
                A("tensor", lambda e, yot=yot, t2k=t2k: e.transpose(t2k, yot[:, 128:256], ident_bf[:]),
                  sig=f"t2_{t}", chan="pe_t2")
                A("scalar", lambda e, t=t, i=i, t2q=t2q, B=B: e.activation(
                    out=QBst[B % 2][:, i * 128:(i + 1) * 128], in_=t2q, func=ACTF.Copy),
                  waits=[f"t2_{t}", (f"qbsp_{B - 2}" if (B >= 2 and i == 0) else None)], sig=f"t2q{t}")
                A("scalar", lambda e, t=t, t2k=t2k: e.activation(out=KB[:, t * 128:(t + 1) * 128], in_=t2k, func=ACTF.Copy),
                  waits=[f"t2_{t}"], sig=f"t2e{t}")

            tm_mm(0)
            tm_evac(0)
            for i in range(4):
                if i + 1 < 4:
                    tm_mm(i + 1)
                    tm_evac(i + 1)
                t2_ops(i)
            A("sync", lambda e, B=B, cs=cs: e.dma_start(out=qb_scr[:, cs], in_=QBst[B % 2][:]),
              waits=[f"t2q{4 * B + 3}"], sig=f"qbsp_{B}", chan=f"d_qbsp{B % 2}", dma=True)

        for t_ in range(3):
            xload(t_)
        norm_stage(0)
        tp_stage(0)
        for B in range(NB):
            if B + 1 < NB:
                norm_stage(B + 1)
            fm_stage(B)
            if B + 1 < NB:
                tp_stage(B + 1)
            tm_stage(B)
            sch.mark(f"p1_{B}")

        P1_DONE_ACT = f"t2e{NT - 1}"
        P1B = [P1_DONE_ACT, f"rp{NT - 1}", f"t2_{NT - 1}", f"xn{NT - 1}", "wout7", f"qbsp_{NB - 2}", f"qbsp_{NB - 1}"] + list(spill_free.values())

        pid = None
        g_i = [0]
        pass_i = [0]

        def block_loads(B):
            p = B % 2
            cs = slice(B * 512, (B + 1) * 512)
            prev = f"blkdone{B - 2}" if B >= 2 else None
            for m in range(2):
                ld("sync", Qb[p][m][0:64, :], qa_scr[m][:, cs], f"lq{B}_{m}b", f"d_blk{p}", waits=[prev, f"qasp{m}_{B}"] + (P1B if B < 2 else []))
                ld("sync", Qb[p][m][64:68, :], qaugb_in[:, cs], f"lq{B}_{m}bb", f"d_blk{p}")
                ld("sync", Qa[p][m][0:64, :], qa_scr[m][:, cs], f"lq{B}_{m}a", f"d_blk{p}")
                ld("sync", Qa[p][m][64:68, :], qauga_in[:, cs], f"lq{B}_{m}aa", f"d_blk{p}")
            for h in range(2):
                ld("sync", QBb[p][h][0:64, :], qb_scr[h * 64:(h + 1) * 64, cs], f"lqb{B}_{h}", f"d_blk{p}", waits=[f"qbsp_{B}"])
                ld("sync", QBb[p][h][64:128, :], qb_scr[h * 64:(h + 1) * 64, cs], f"lqb{B}_{h}d", f"d_blk{p}")
                ld("sync", SGB[p][h][:], sgb_scr[h * 64:(h + 1) * 64, cs], f"lsgb{B}_{h}", f"d_blk{p}", waits=[f"gbsp_{B}"])
            ld("sync", SGA[p][:], sga_scr[:, cs], f"blkld{B}", f"d_blk{p}", waits=[f"gasp_{B}"])

        def qk_ops(B, mp, c, dst):
            p = B % 2
            ops = []
            kc = slice(c * 128, (c + 1) * 128)
            if mp < 2:
                m = mp
                if c < 4 * B:
                    ops.append(lambda e: e.matmul(dst, lhsT=KA[m][0:68, kc], rhs=Qb[p][m][:, :], start=True, stop=True))
                elif c > 4 * B + 3:
                    ops.append(lambda e: e.matmul(dst, lhsT=KA[m][0:68, kc], rhs=Qa[p][m][:, :], start=True, stop=True))
                else:
                    j = c - 4 * B
                    if j > 0:
                        ops.append(lambda e: e.matmul(dst[:, 0:128 * j], lhsT=KA[m][0:68, kc], rhs=Qa[p][m][:, 0:128 * j],
                                                      start=True, stop=True))
                    ops.append(lambda e: e.matmul(dst[:, 128 * j:128 * (j + 1)], lhsT=KA[m][0:64, kc],
                                                  rhs=Qb[p][m][0:64, 128 * j:128 * (j + 1)], start=True, stop=False))
                    ops.append(lambda e: e.matmul(dst[:, 128 * j:128 * (j + 1)], lhsT=ident_bf[:], rhs=dtile[:],
                                                  start=False, stop=True))
                    if j < 3:
                        ops.append(lambda e: e.matmul(dst[:, 128 * (j + 1):512], lhsT=KA[m][0:68, kc],
                                                      rhs=Qb[p][m][:, 128 * (j + 1):512], start=True, stop=True))
            else:
                h = mp - 2
                r0 = 64 * (c % 2)
                ops.append(lambda e: e.matmul(dst, lhsT=KB[r0:r0 + 64, kc], rhs=QBb[p][h][r0:r0 + 64, :], start=True, stop=True))
            return ops

        def p3_pe(qb, extra=()):
            p = qb % 2
            for half in range(2):
                for ch in range(8):
                    A("tensor", lambda e, half=half, ch=ch, p=p: e.matmul(
                        YPS[:, half * 512:(half + 1) * 512], lhsT=Gg[p][:, ch, :], rhs=Wout[:, ch, half * 512:(half + 1) * 512],
                        start=(ch == 0), stop=(ch == 7)),
                      waits=[f"gg{qb}", "wout7", (f"z0_{qb - 1}" if qb >= 1 else "mod_ev2")] + list(extra),
                      sig=(f"y{qb}" if (half == 1 and ch == 7) else None), chan="pe_p3")

        def p3_dve(qb):
            p = qb % 2
            A("vector", lambda e: e.tensor_tensor(out=zt[:], in0=YPS[:], in1=gate_bc[:], op=ALU.mult),
              waits=[f"y{qb}", "mod_ev2"], sig=f"z0_{qb}")
            A("vector", lambda e, p=p: e.tensor_tensor(out=zt[:], in0=zt[:], in1=xot[p][:], op=ALU.add),
              waits=[f"z0_{qb}", f"xo{qb}"], sig=f"z1_{qb}")
            A("vector", lambda e: e.tensor_tensor(out=ot[qb % 2], in0=zt, in1=zt, op=ALU.mult), waits=[f"z1_{qb}", (f"st{qb - 2}" if qb >= 2 else None)], sig=f"z2_{qb}")
            A("vector", lambda e: e.tensor_reduce(out=zss[:, 0:1], in_=ot[qb % 2], axis=AX.X, op=ALU.add), waits=[f"z2_{qb}"], sig=f"z3_{qb}")
            A("vector", lambda e: e.tensor_scalar(out=zss[:, 1:2], in0=zss[:, 0:1], scalar1=1.0 / D, scalar2=EPS,
                                                  op0=ALU.mult, op1=ALU.add), waits=[f"z3_{qb}"], sig=f"z4_{qb}")
            A("gpsimd", lambda e: e.tensor_tensor(out=zss[:, 1:2], in0=zss[:, 1:2], in1=NH(zss[:, 1:2]), op=ALU.pow),
              waits=[f"z4_{qb}"], sig=f"z5_{qb}")
            A("vector", lambda e, p=p: e.scalar_tensor_tensor(out=ot[p][:], in0=zt[:], scalar=zss[:, 1:2], in1=fnw[:],
                                                              op0=ALU.mult, op1=ALU.mult),
              waits=[f"z5_{qb}", CONST_ALL, (f"st{qb - 2}" if qb >= 2 else None)], sig=f"z6_{qb}")
            A("sync", lambda e, p=p: e.dma_start(out=out[qb * 128:(qb + 1) * 128, :], in_=ot[p][:]),
              waits=[f"z6_{qb}"], sig=f"st{qb}", chan=f"d_st{p}", dma=True)

        def p3_loads(qb):
            p = qb % 2

            def gather(e, qb=qb, p=p):
                j = sch.pidj
                src = exout[qb].ap().rearrange("(c f) (t q) -> t f c q", f=128, q=128)
                return e.dma_start(out=Gg[p][:], in_=src[bass.ds(j, 1)].rearrange("o f c q -> (o f) c q"))
            A("sync", gather, waits=[f"cc{qb}", (f"y{qb - 2}" if qb >= 2 else None)], sig=f"gg{qb}", chan=f"d_gg{p}", dma=True)
            ld("sync", xot[p][:], xo[qb * 128:(qb + 1) * 128, :], f"xo{qb}", f"d_xo{p}", waits=[f"z1_{qb - 2}" if qb >= 2 else None])

        block_loads(0)
        block_loads(1)
        deferred = []
        last_fin_read = ["mod_ev2"]
        last_z0 = [None]
        FIN = YPS[:, 0:512]
        for B in range(NB):
            p = B % 2
            for mp in range(4):
                ps_i = pass_i[0]
                pass_i[0] += 1
                isA = mp < 2
                M = 128 if isA else 64
                gl = []
                for cp in range(32):
                    g = g_i[0]
                    g_i[0] += 1
                    gl.append(g)
                for idx in range(33):
                    for dd in [d_ for d_ in deferred if d_[0] == idx]:
                        dd[1]()
                        deferred.remove(dd)
                    if idx < 32:
                        g = gl[idx]
                        spd = SP[g % 2]
                        first = True
                        for sub in range(2):
                            c = 2 * idx + sub
                            ops = qk_ops(B, mp, c, spd[:, sub * 512:(sub + 1) * 512])
                            for oi, fn in enumerate(ops):
                                lastop = (sub == 1 and oi == len(ops) - 1)
                                w = []
                                if first:
                                    w = [f"exp{g - 2}" if g >= 2 else None, f"blkld{B}", P1_DONE_ACT, CONST_ALL] + (P1B if g == 0 else [])
                                    first = False
                                A("tensor", fn, waits=w, sig=(f"qk{g}" if lastop else None), chan="pe_qk")
                        A("scalar", lambda e, g=g, spd=spd: e.activation(out=PT[g % 3][:], in_=spd[:], func=ACTF.Exp, scale=0.125),
                          waits=[f"qk{g}", (f"pv{g - 3}" if g >= 3 else None)] + (P1B if g == 0 else []), sig=f"exp{g}", chan="act_exp")
                    if idx >= 1:
                        g = gl[idx - 1]
                        pt = PT[g % 3]
                        for sub in range(2):
                            c = 2 * (idx - 1) + sub
                            st = (c == 0)
                            sp_ = (c == NT - 1)
                            Vl = VA[:, c, :] if isA else VB[:, c, :]
                            w = [f"exp{g}"] if sub == 0 else []
                            if st:
                                w.append(f"evac{ps_i - 1}" if ps_i >= 1 else None)
                            A("tensor", lambda e, Vl=Vl, pt=pt, sub=sub, st=st, sp_=sp_, M=M: e.matmul(
                                ACCO[0:M, :], lhsT=Vl, rhs=pt[:, sub * 512:(sub + 1) * 512], start=st, stop=sp_), waits=w)
                        for sub in range(2):
                            c = 2 * (idx - 1) + sub
                            st = (c < 2)
                            sp_ = (c >= NT - 2)
                            A("tensor", lambda e, pt=pt, sub=sub, st=st, sp_=sp_: e.matmul(
                                ACCS[32 * sub:32 * sub + 1, :], lhsT=ones_bf[:, 0:1], rhs=pt[:, sub * 512:(sub + 1) * 512],
                                start=st, stop=sp_, tile_position=(0, 32 * sub)),
                              sig=(f"pv{g}" if sub == 1 else None), chan="pe_pv")
                LASTPV = f"pv{gl[-1]}"
                A("vector", lambda e, M=M: e.tensor_copy(out=Ocp[0:M, :], in_=ACCO[0:M, :]), waits=[LASTPV] + (P1B if ps_i == 0 else []))
                A("vector", lambda e: e.tensor_copy(out=s64[0:1, :], in_=ACCS[0:1, :]))
                A("vector", lambda e: e.tensor_copy(out=s64[32:33, :], in_=ACCS[32:33, :]), sig=f"evac{ps_i}")
                if mp == 0 and B >= 1:
                    p3_pe(B - 1, [last_fin_read[0]])
                    p3_dve(B - 1)
                z0dep = f"z0_{B - 1}" if (B >= 1 and mp == 0) else last_z0[0]
                if B >= 1 and mp == 0:
                    last_z0[0] = f"z0_{B - 1}"
                deferred.append((2, lambda M=M, ps_i=ps_i, z0dep=z0dep: A(
                    "tensor", lambda e: e.matmul(FIN[0:M, :], lhsT=sel[:, 0:M], rhs=s64[:, :], start=True, stop=True),
                    waits=[f"evac{ps_i}", CONST_ALL, z0dep], sig=f"sbc{ps_i}", chan="pe_fin")))
                A("vector", lambda e, M=M: e.reciprocal(out=fR[0:M, :], in_=FIN[0:M, :]), waits=[f"sbc{ps_i}"], sig=f"f0_{ps_i}")
                last_fin_read[0] = f"f0_{ps_i}"
                if mp == 0:
                    A("vector", lambda e: e.tensor_tensor(out=fT1[:], in0=Ocp[:], in1=fR[:], op=ALU.mult),
                      waits=[f"f0_{ps_i}"], sig=f"fin{ps_i}")
                elif mp == 1:
                    A("vector", lambda e: e.tensor_tensor(out=fT2[:], in0=Ocp[:], in1=fR[:], op=ALU.mult),
                      waits=[f"f0_{ps_i}"], sig=f"f1_{ps_i}")
                    A("vector", lambda e: e.scalar_tensor_tensor(out=fD[:], in0=fT2[:], scalar=nlam[:, 0:1], in1=fT1[:],
                                                                 op0=ALU.mult, op1=ALU.add),
                      waits=[f"f1_{ps_i}", "nlam"], sig=f"f2_{ps_i}")
                    A("vector", lambda e: e.tensor_tensor(out=fD2[:], in0=fD[:], in1=fD[:], op=ALU.mult),
                      waits=[f"f2_{ps_i}"], sig=f"f3_{ps_i}")
                    deferred.append((8, lambda ps_i=ps_i: A(
                        "tensor", lambda e: e.matmul(FIN[:], lhsT=ones_f[:], rhs=fD2[:], start=True, stop=True),
                        waits=[f"f3_{ps_i}", "ones"], sig=f"ssn{ps_i}", chan="pe_fin")))
                    A("vector", lambda e: e.tensor_scalar(out=fRS[:], in0=FIN[:], scalar1=1.0 / 128, scalar2=EPS,
                                                          op0=ALU.mult, op1=ALU.add), waits=[f"ssn{ps_i}"], sig=f"f4_{ps_i}")
                    last_fin_read[0] = f"f4_{ps_i}"
                    deferred.append((12, lambda ps_i=ps_i: A(
                        "scalar", lambda e: e.activation(out=fRS[:], in_=fRS[:], func=ACTF.Ln), waits=[f"f4_{ps_i}"], sig=f"f4b_{ps_i}")))
                    deferred.append((12, lambda ps_i=ps_i: A(
                        "scalar", lambda e: e.activation(out=fRS[:], in_=fRS[:], func=ACTF.Exp, scale=-0.5),
                        waits=[f"f4b_{ps_i}"], sig=f"fin{ps_i}")))
                    A("vector", lambda e: e.scalar_tensor_tensor(out=fD[:], in0=fD[:], scalar=subw[:, 0:1], in1=fRS[:],
                                                                 op0=ALU.mult, op1=ALU.mult),
                      waits=[f"fin{ps_i}", "subw"], sig=f"f5_{ps_i}")
                    A("vector", lambda e, p=p: e.tensor_tensor(out=GA[p][:], in0=fD[:], in1=SGA[p][:], op=ALU.mult),
                      waits=[f"f5_{ps_i}", f"blkld{B}", (f"exw{B - 2}" if B >= 2 else None)], sig=f"ga{B}")
                else:
                    h = mp - 2
                    A("vector", lambda e: e.tensor_tensor(out=fT1[0:64, :], in0=Ocp[0:64, :], in1=fR[0:64, :], op=ALU.mult),
                      waits=[f"f0_{ps_i}"], sig=f"fin{ps_i}")
                    A("vector", lambda e, p=p, h=h: e.tensor_tensor(out=GB[p][h][:], in0=fT1[0:64, :], in1=SGB[p][h][:], op=ALU.mult),
                      waits=[f"fin{ps_i}", f"blkld{B}", (f"exw{B - 2}" if B >= 2 else None)], sig=f"gb{B}_{h}")
                if B == NB - 1 and mp == 3:
                    for _, fn_ in deferred:
                        fn_()
                    deferred.clear()
            sch.ev[f"blkdone{B}"] = sch.ev[f"gb{B}_1"]
            exi = exin[B].ap()
            A("gpsimd", lambda e, p=p, exi=exi: e.dma_start(out=exi[0:128, :], in_=GA[p][:]), waits=[f"ga{B}"],
              sig=f"exw{B}_0", chan=f"d_exw{p}", dma=True)
            A("gpsimd", lambda e, p=p, exi=exi: e.dma_start(out=exi[128:192, :], in_=GB[p][0][:]), waits=[f"gb{B}_0"],
              sig=f"exw{B}_1", chan=f"d_exw{p}", dma=True)
            A("gpsimd", lambda e, p=p, exi=exi: e.dma_start(out=exi[192:256, :], in_=GB[p][1][:]), waits=[f"gb{B}_1"],
              sig=f"exw{B}", chan=f"d_exw{p}", dma=True)
            A("gpsimd", lambda e, B=B: e.collective_compute(
                "AllGather", ALU.bypass, replica_groups=[[0, 1, 2, 3], [4, 5, 6, 7]],
                ins=[exin[B].ap()], outs=[exout[B].ap()]), waits=[f"exw{B}"], sig=f"cc{B}", chan="cc")
            sch.mark(f"p2a_{B}")
            if B + 2 < NB:
                block_loads(B + 2)
            p3_loads(B)
            sch.mark(f"p2_{B}")
        p3_pe(NB - 1, [last_fin_read[0]])
        p3_dve(NB - 1)
        A("sync", lambda e: e.nop(), waits=[f"st{NB - 2}", f"st{NB - 1}"])

        sch.limit = trunc
        chans = sorted(sch.cnt.keys())
        sems = {ch: es.enter_context(nc.semaphore(ch)) for ch in chans}
        with nc.Block() as block:
            @block.sync
            def _(e):
                sch.emit("sync", e, sems)

            @block.scalar
            def _(e):
                sch.emit("scalar", e, sems)

            @block.gpsimd
            def _(e):
                sch.emit("gpsimd", e, sems)

            @block.vector
            def _(e):
                sch.emit("vector", e, sems)

            @block.tensor
            def _(e):
                sch.emit("tensor", e, sems)
    return nc


def _host_inputs(inputs):
    f32 = np.float32
    bf = ml_dtypes.bfloat16
    x = np.asarray(inputs["x"], f32)
    c = np.asarray(inputs["c"], f32)
    w_ada = np.ascontiguousarray(np.asarray(inputs["w_ada"], f32)[0])
    b_ada = np.asarray(inputs["b_ada"], f32)[0]
    norm_w = np.asarray(inputs["norm_w"], f32)[0]
    w_in = np.asarray(inputs["w_in"], f32)[0]
    w_out = np.asarray(inputs["w_out"], f32)[0]
    lq1 = np.asarray(inputs["lambda_q1"], f32)[0]
    lk1 = np.asarray(inputs["lambda_k1"], f32)[0]
    lq2 = np.asarray(inputs["lambda_q2"], f32)[0]
    lk2 = np.asarray(inputs["lambda_k2"], f32)[0]
    subln = np.asarray(inputs["subln_w"], f32)[0]
    qw = np.asarray(inputs["q_norm_w"], f32)[0]
    kw = np.asarray(inputs["k_norm_w"], f32)[0]
    fnw = np.asarray(inputs["final_norm_w"], f32)

    def bc(v):
        return np.ascontiguousarray(np.broadcast_to(v[None, :], (128, v.shape[0])))

    tok = np.arange(S)
    row = (tok // 64).astype(f32)
    col = (tok % 64).astype(f32)
    freqs = (f32(1.0) / (f32(10000.0) ** (np.arange(16, dtype=f32) * f32(2.0) / f32(32)))).astype(f32)
    ang_r = (row[:, None] * freqs[None, :]).astype(f32)
    ang_c = (col[:, None] * freqs[None, :]).astype(f32)
    cr, sr, cc, sn_c = np.cos(ang_r), np.sin(ang_r), np.cos(ang_c), np.sin(ang_c)
    cos64 = np.concatenate([cr, cr, cc, cc], axis=1)
    sin64 = np.concatenate([-sr, sr, -sn_c, sn_c], axis=1)
    rope = np.ascontiguousarray(np.concatenate([cos64, sin64], axis=1).astype(f32).reshape(NT, 128, 128))
    ident = np.eye(128, dtype=f32)
    selm = np.zeros((64, 128), f32)
    selm[0, :] = 1.0
    selm[32, :] = 1.0
    w_out_rows = np.concatenate([np.concatenate([np.arange(r * 128, (r + 1) * 128),
                                                 512 + np.arange(r * 128, (r + 1) * 128)]) for r in range(4)])
    w_out_p = np.ascontiguousarray(w_out[w_out_rows])
    shared = dict(
        w_ada=w_ada, bada=bc(b_ada), normwT=np.ascontiguousarray(norm_w.reshape(8, 128).T),
        w_out=w_out_p, lam_in=bc(np.concatenate([lq1, lk1, lq2, lk2])), sublnT=np.ascontiguousarray(subln[:, None]),
        wn_in=bc(np.concatenate([qw, qw, kw])), fnw_in=bc(fnw), ident_bf=ident.astype(bf), ident_f=ident, rope=rope, sel=selm,
    )
    maps = []
    pos = np.arange(S)
    a_, r_ = (pos // 128).astype(f32), (pos % 128).astype(f32)
    for cid in range(8):
        b, j = cid // 4, cid % 4
        g = j // 2
        cols = np.concatenate([
            j * 128 + np.arange(128),
            512 + j * 128 + np.arange(128),
            1536 + j * 128 + np.arange(128),
            2816 + 2 * j * 64 + np.arange(128),
            2048 + 2 * j * 64 + np.arange(128),
            2560 + g * 64 + np.arange(64),
            2688 + g * 64 + np.arange(64),
            1024 + j * 128 + np.arange(128),
        ])
        sig = f32(8.0 * 2.0 ** (-2.0 * (j + 1)))
        kaug = np.stack([sig * r_, sig * 128 * a_, np.ones(S, f32), np.ones(S, f32)])
        qaugb = np.stack([np.ones(S, f32), np.ones(S, f32), -sig * 128 * a_, -sig * r_])
        ii = np.arange(128, dtype=f32)
        dt_ = -sig * np.abs(ii[None, :] - ii[:, None])
        xb = np.ascontiguousarray(x[b])
        m = dict(shared)
        m.update(
            x=xb, xo=np.ascontiguousarray(xb.reshape(NB, 4, 128, D)[:, j].reshape(NB * 128, D)),
            cT=np.ascontiguousarray(c[b].reshape(8, 128).T), w_in=np.ascontiguousarray(w_in[:, cols]),
            kaug=kaug.astype(bf), qaugb=qaugb.astype(bf), qauga=(-qaugb).astype(bf), dtile=dt_.astype(bf),
        )
        maps.append(m)
    return maps


_NC = None


def kernel(**inputs):
    global _NC
    if _NC is None:
        _NC = build_program()
    maps = _host_inputs(inputs)
    res = run_bass_kernel_spmd(_NC, maps, core_ids=list(range(8)))
    outp = np.empty((2, NB, 4, 128, D), np.float32)
    for cid in range(8):
        b, j = cid // 4, cid % 4
        outp[b, :, j] = np.asarray(res.results[cid]["out"], np.float32).reshape(NB, 128, D)
    return outp.reshape(2, S, D)
```

```python
import math
import numpy as np
import ml_dtypes
import concourse.bass as bass
import concourse.mybir as mybir
from concourse.bass_utils import run_bass_kernel_spmd

F32 = mybir.dt.float32
BF16 = mybir.dt.bfloat16
ALU = mybir.AluOpType
ACTF = mybir.ActivationFunctionType
AX = mybir.AxisListType

S = 8192
D = 1024
NT = S // 128
NB = S // 512
EPS = 1e-6
LAM_INIT = 0.8 - 0.6 * math.exp(0.0)
ENGS = ["sync", "scalar", "gpsimd", "vector", "tensor"]


class Sched:
    def __init__(self):
        self.ops = {e: [] for e in ENGS}
        self.cnt = {}
        self.ev = {}
        self.chan_eng = {}
        self.marks = {}
        self.seq = []
        self.limit = None

    def mark(self, name):
        self.marks[name] = {e: len(self.ops[e]) for e in ENGS}

    def add(self, eng, fn, waits=(), sig=None, chan=None, dma=False):
        ch = None
        inc = 16 if dma else 1
        if sig is not None:
            ch = chan or ("c_" + eng)
            assert self.chan_eng.setdefault(ch, eng) == eng, (ch, eng)
            self.cnt[ch] = self.cnt.get(ch, 0) + inc
            assert sig not in self.ev, sig
            self.ev[sig] = (ch, self.cnt[ch])
        self.ops[eng].append((fn, tuple(w for w in waits if w is not None), sig, ch, inc))
        self.seq.append(eng)

    def emit(self, ename, eng, sems):
        waited = {}
        if ename == "sync":
            self.pidj = eng.partition_id() % 4
        ops = self.ops[ename]
        if self.limit is not None:
            if self.limit.startswith("n:"):
                ops = ops[:self.seq[:int(self.limit[2:])].count(ename)]
            else:
                ops = ops[:self.marks[self.limit][ename]]
        for fn, waits, sig, ch, inc in ops:
            for w in waits:
                wch, val = self.ev[w]
                if waited.get(wch, 0) >= val:
                    continue
                eng.wait_ge(sems[wch], val)
                waited[wch] = val
            ins = fn(eng)
            if sig is not None:
                ins.then_inc(sems[ch], inc)


def build_program(trunc=None):
    nc = bass.Bass("TRN2", target_bir_lowering=False)

    def din(name, shape, dt=F32):
        return nc.dram_tensor(name, list(shape), dt, kind="ExternalInput").ap()

    x = din("x", [S, D])
    xo = din("xo", [NB * 128, D])
    cT = din("cT", [128, 8])
    w_ada = din("w_ada", [D, 3 * D])
    bada = din("bada", [128, 3 * D])
    normwT = din("normwT", [128, 8])
    w_in = din("w_in", [D, 896])
    w_out = din("w_out", [D, D])
    lam_in = din("lam_in", [128, 256])
    sublnT = din("sublnT", [128, 1])
    wn_in = din("wn_in", [128, 192])
    fnw_in = din("fnw_in", [128, D])
    ident_bf_in = din("ident_bf", [128, 128], BF16)
    ident_f_in = din("ident_f", [128, 128])
    rope_in = din("rope", [NT, 128, 128])
    kaug_in = din("kaug", [4, S], BF16)
    qaugb_in = din("qaugb", [4, S], BF16)
    qauga_in = din("qauga", [4, S], BF16)
    dtile_in = din("dtile", [128, 128], BF16)
    sel_in = din("sel", [64, 128])
    out = nc.dram_tensor("out", [NB * 128, D], F32, kind="ExternalOutput").ap()

    qa_scr = [nc.dram_tensor(f"qa_scr{m}", [64, S], BF16).ap() for m in range(2)]
    qb_scr = nc.dram_tensor("qb_scr", [128, S], BF16).ap()
    sga_scr = nc.dram_tensor("sga_scr", [128, S], BF16).ap()
    sgb_scr = nc.dram_tensor("sgb_scr", [128, S], BF16).ap()
    exin = [nc.dram_tensor(f"exin{i}", [256, 512], BF16) for i in range(NB)]
    exout = [nc.dram_tensor(f"exout{i}", [1024, 512], BF16) for i in range(NB)]

    sch = Sched()
    A = sch.add
    from contextlib import ExitStack
    es = ExitStack()

    def sb(name, shape, dt=F32):
        return es.enter_context(nc.sbuf_tensor("s_" + name, list(shape), dt))

    def ps(name, shape, dt=F32):
        return es.enter_context(nc.psum_tensor("p_" + name, list(shape), dt))

    with es:
        KA = [sb(f"KA{m}", [68, S], BF16) for m in range(2)]
        KB = sb("KB", [128, S], BF16)
        VA = sb("VA", [128, NT, 128], BF16)
        VB = sb("VB", [128, NT, 64], BF16)
        Wout = sb("Wout", [128, 8, D], BF16)
        hT = [sb(f"hT{i}", [128, 8, 512], BF16) for i in range(2)]
        h0f = hT[0][:].rearrange("p a b -> p (a b)").bitcast(F32)
        h1f = hT[1][:].rearrange("p a b -> p (a b)").bitcast(F32)
        mtmp = h0f[:, 0:1024]
        prod = h0f[:, 1024:2048].rearrange("p (a b) -> p a b", a=8)
        badat = h1f[:, 0:1024]
        scb = h1f[:, 1024:2048].rearrange("p (a b) -> p a b", a=8)
        ropet = [sb(f"ropet{i}", [128, 128]) for i in range(3)]
        gate_bc = sb("gate_bc", [128, D])
        fnw = sb("fnw", [128, D])
        small = sb("small", [128, 64])
        sc = sb("sc", [128, 8])
        shiftT = sb("shiftT", [128, 8])
        scaleT = sb("scaleT", [128, 8])
        aT = sb("aT", [128, 8])
        normw_sb = sb("normw_sb", [128, 8])
        cT_sb = sb("cT_sb", [128, 8])
        lam_sb = sb("lam_sb", [128, 256])
        lamp = sb("lamp", [128, 128])
        nlam = sb("nlam", [128, 1])
        subw = sb("subw", [128, 1])
        WN = sb("WN", [128, 192])
        ident_bf = sb("ident_bf", [128, 128], BF16)
        ident_f = sb("ident_f", [128, 128])
        ones_bf = sb("ones_bf", [128, 128], BF16)
        ones_f = sb("ones_f", [128, 128])
        dtile = sb("dtile", [128, 128], BF16)
        sel = sb("sel", [64, 128])
        s64 = sb("s64", [64, 512])
        ss = sb("ss", [128, NT])
        rstd = sb("rstd", [128, NT])
        ss3 = sb("ss3", [128, 3])
        rstd3 = sb("rstd3", [128, 3])
        yo = [sb(f"yo{i}", [128, 256], BF16) for i in range(2)]
        UN = 32768
        U = sb("U", [128, UN], BF16)
        cur = [0]

        def cv(shape, dt=BF16, rows=128):
            n = 1
            for d_ in shape[1:]:
                n *= d_
            nb = n * (2 if dt == F32 else 1)
            o = cur[0]
            cur[0] += nb
            assert cur[0] <= UN, cur[0]
            v = U[0:shape[0], o:o + nb]
            if dt == F32:
                v = v.bitcast(F32)
            if len(shape) == 3:
                v = v.rearrange("p (a b) -> p a b", a=shape[1])
            return v
        xn = [cv([128, D]) for i in range(8)]
        xt = [cv([128, D], F32) for i in range(3)]
        wst = [cv([128, 1024], F32) for i in range(2)]
        Win = cv([128, 8, 896])
        qst = [cv([128, 512]) for i in range(3)]
        gtmp = [cv([128, 512], F32) for i in range(2)]
        QBst = [cv([128, 512]) for i in range(2)]
        tq = cv([128, 192], F32)
        tsq = cv([128, 192], F32)
        ty = cv([128, 192], F32)
        tA = cv([128, 192], F32)
        tB = cv([128, 192], F32)
        print("phase1 union bytes", cur[0] * 2)
        cur[0] = 0
        Qb = [[cv([68, 512]) for m in range(2)] for p in range(2)]
        Qa = [[cv([68, 512]) for m in range(2)] for p in range(2)]
        QBb = [[cv([128, 512]) for h in range(2)] for p in range(2)]
        SGA = [cv([128, 512]) for p in range(2)]
        SGB = [[cv([64, 512]) for h in range(2)] for p in range(2)]
        PT = [cv([128, 1024]) for i in range(3)]
        fR = cv([128, 512], F32)
        fT1 = cv([128, 512], F32)
        fT2 = cv([128, 512], F32)
        fD = cv([128, 512], F32)
        Ocp = cv([128, 512], F32)
        fD2 = fT2
        fRS = fR
        GA = [cv([128, 512]) for p in range(2)]
        GB = [[cv([64, 512]) for h in range(2)] for p in range(2)]
        Gg = [cv([128, 8, 128]) for p in range(2)]
        xot = [cv([128, D], F32) for p in range(2)]
        zt = cv([128, D], F32)
        ot = [cv([128, D], F32) for p in range(2)]
        zss = sb("zss", [128, 2])
        neghalf = sb("neghalf", [128, 512])

        def NH(ap):
            return neghalf[0:ap.shape[0], 0:ap.shape[1]]
        print("phase2 union bytes", cur[0] * 2)

        SP = [ps(f"SP{i}", [128, 1024]) for i in range(2)]
        ACCO = ps("ACCO", [128, 512])
        ACCS = ps("ACCS", [128, 512])
        YPS = ps("YPS", [128, 1024])
        TPb = [ACCO[:].bitcast(BF16)[:, 0:512], YPS[:, 0:512].bitcast(BF16)[:, 0:512]]
        T2b = [ACCS[:].bitcast(BF16)[:, 0:256], YPS[:, 512:1024].bitcast(BF16)[:, 0:256]]

        def ld(eng, dst, src, sig, chan, waits=()):
            A(eng, lambda e, d=dst, s=src: e.dma_start(out=d, in_=s), waits=waits, sig=sig, chan=chan, dma=True)

        consts = [
            (cT_sb[:], cT), (normw_sb[:], normwT), (lam_sb[:], lam_in), (subw[:], sublnT),
            (WN[:], wn_in), (fnw[:], fnw_in), (ident_bf[:], ident_bf_in), (ident_f[:], ident_f_in),
            (dtile[:], dtile_in), (sel[:], sel_in), (KA[0][64:68, :], kaug_in), (KA[1][64:68, :], kaug_in),
        ]
        for i, (d_, s_) in enumerate(consts):
            ld("sync", d_, s_, f"const{i}", "d_const")
        CONST_ALL = f"const{len(consts) - 1}"

        A("vector", lambda e: e.memset(ones_bf[:], 1.0))
        A("vector", lambda e: e.memset(ss[:], 0.0))
        A("vector", lambda e: e.memset(s64[:], 0.0))
        A("gpsimd", lambda e: e.memset(neghalf[:], -0.5))
        A("vector", lambda e: e.memset(ones_f[:], 1.0), sig="ones")
        A("scalar", lambda e: e.activation(out=sc[:], in_=cT_sb[:], func=ACTF.Exp, scale=-1.0),
          waits=[CONST_ALL], sig="p0_e")
        A("vector", lambda e: e.tensor_scalar(out=sc[:], in0=sc[:], scalar1=1.0, scalar2=None, op0=ALU.add),
          waits=["p0_e"], sig="p0_a")
        A("vector", lambda e: e.reciprocal(out=sc[:], in_=sc[:]), waits=["p0_a"], sig="p0_b")
        A("vector", lambda e: e.tensor_tensor(out=sc[:], in0=sc[:], in1=cT_sb[:], op=ALU.mult),
          waits=["p0_b", CONST_ALL], sig="p0_c")
        for kc in range(8):
            A("vector", lambda e, kc=kc: e.tensor_scalar(out=scb[:, kc, :], in0=ones_f[:], scalar1=sc[:, kc:kc + 1],
                                                         scalar2=None, op0=ALU.mult),
              waits=["p0_c", "ones"], sig=f"scb{kc}")
        A("vector", lambda e: e.tensor_tensor(out=lamp[:, 0:64], in0=lam_sb[:, 0:64], in1=lam_sb[:, 64:128], op=ALU.mult),
          waits=[CONST_ALL])
        A("vector", lambda e: e.tensor_tensor(out=lamp[:, 64:128], in0=lam_sb[:, 128:192], in1=lam_sb[:, 192:256], op=ALU.mult),
          sig="lam_p")
        A("vector", lambda e: e.tensor_reduce(out=small[:, 0:2], in_=lamp[:].rearrange("p (a b) -> p a b", a=2),
                                              axis=AX.X, op=ALU.add), waits=["lam_p"], sig="lam_s")
        A("scalar", lambda e: e.activation(out=small[:, 2:4], in_=small[:, 0:2], func=ACTF.Exp), waits=["lam_s"], sig="lam_e")
        A("vector", lambda e: e.tensor_tensor(out=nlam[:], in0=small[:, 3:4], in1=small[:, 2:3], op=ALU.subtract),
          waits=["lam_e"], sig="lam_d")
        A("vector", lambda e: e.tensor_scalar(out=nlam[:], in0=nlam[:], scalar1=-LAM_INIT, scalar2=None, op0=ALU.add),
          waits=["lam_d"], sig="nlam")
        A("vector", lambda e: e.tensor_scalar(out=subw[:], in0=subw[:], scalar1=1.0 - LAM_INIT, scalar2=None, op0=ALU.mult),
          waits=[CONST_ALL], sig="subw")

        stage_i = [0]
        stage_free = {}

        def stage_load(src_ap, width):
            i = stage_i[0]
            stage_i[0] += 1
            slot = i % 2
            sig = f"stg{i}"
            ld("sync" if slot == 0 else "scalar", wst[slot][:, 0:width], src_ap, sig, f"d_wst{slot}", waits=[stage_free.get(slot)])
            return slot, sig

        wav = w_ada.rearrange("(kc p) n -> kc p n", p=128)
        for cg in range(3):
            A("sync", lambda e, cg=cg: e.dma_start(out=badat[:], in_=bada[:, cg * 1024:(cg + 1) * 1024]),
              waits=[f"mod_ev{cg - 1}" if cg > 0 else None], sig=f"bada{cg}", chan="d_bada", dma=True)
            for kc in range(8):
                slot, sig = stage_load(wav[kc, :, cg * 1024:(cg + 1) * 1024], 1024)
                for half in range(2):
                    last = half == 1
                    A("tensor", lambda e, kc=kc, half=half, slot=slot: e.matmul(
                        YPS[:, half * 512:(half + 1) * 512], lhsT=scb[:, kc, :],
                        rhs=wst[slot][:, half * 512:(half + 1) * 512], start=(kc == 0), stop=(kc == 7)),
                      waits=[sig, f"scb{kc}", (f"mod_ev{cg - 1}" if (cg > 0 and kc == 0) else None)],
                      sig=(f"modmm{cg}_{kc}" if last else None), chan="pe_p0")
                stage_free[slot] = f"modmm{cg}_{kc}"
            dst = gate_bc if cg == 2 else mtmp
            A("vector", lambda e, dst=dst: e.tensor_tensor(out=dst[:], in0=YPS[:], in1=badat[:], op=ALU.add),
              waits=[f"modmm{cg}_7", f"bada{cg}", (f"mod_red{cg - 1}" if cg > 0 else None)], sig=f"mod_ev{cg}")
            if cg < 2:
                for kc in range(8):
                    A("vector", lambda e, kc=kc: e.tensor_tensor(out=prod[:, kc, :], in0=mtmp[:, kc * 128:(kc + 1) * 128],
                                                                 in1=ident_f[:], op=ALU.mult),
                      waits=[f"mod_ev{cg}", CONST_ALL], sig=f"mod_pr{cg}_{kc}")
                dstT = shiftT if cg == 0 else scaleT
                A("vector", lambda e, dstT=dstT: e.tensor_reduce(out=dstT[:], in_=prod[:], axis=AX.X, op=ALU.add),
                  waits=[f"mod_pr{cg}_7"], sig=f"mod_red{cg}")
        A("vector", lambda e: e.tensor_scalar(out=aT[:], in0=scaleT[:], scalar1=1.0, scalar2=None, op0=ALU.add),
          waits=["mod_red1"], sig="aT0")
        A("vector", lambda e: e.tensor_tensor(out=aT[:], in0=aT[:], in1=normw_sb[:], op=ALU.mult),
          waits=["aT0", CONST_ALL], sig="aT")

        sch.mark("p0a")
        win_v = w_in.rearrange("(kc p) n -> kc p n", p=128)
        for kc in range(8):
            slot, sig = stage_load(win_v[kc], 896)
            A("gpsimd", lambda e, kc=kc, slot=slot: e.tensor_copy(out=Win[:, kc, :], in_=wst[slot][:, 0:896]),
              waits=[sig], sig=f"win{kc}", chan="pool_w")
            stage_free[slot] = f"win{kc}"
        wout_v = w_out.rearrange("(kc p) n -> kc p n", p=128)
        for kc in range(8):
            slot, sig = stage_load(wout_v[kc], 1024)
            A("gpsimd", lambda e, kc=kc, slot=slot: e.tensor_copy(out=Wout[:, kc, :], in_=wst[slot][:, 0:1024]),
              waits=[sig], sig=f"wout{kc}", chan="pool_w")
            stage_free[slot] = f"wout{kc}"

        sch.mark("p0")
        xv = x.rearrange("(t p) d -> t p d", p=128)
        ldx = {}

        def issue_xload(t):
            if t >= NT:
                return
            ld("scalar", xt[t % 2], xv[t], f"xld{t}", f"d_xt{t % 2}", waits=[f"xn{t - 2}" if t >= 2 else None])

        fm_i = [0]
        spill_i = [0]
        spill_free = {}
        gt_i = [0]

        def spill(src_fn_eng, src_emit, rows, dram_ap, waits, tag):
            i = spill_i[0]
            spill_i[0] += 1
            slot = i % 3
            A(src_fn_eng, lambda e, slot=slot: src_emit(e, qst[slot][0:rows, :]),
              waits=list(waits) + [spill_free.get(slot)], sig=f"spw{i}", chan=f"c_{src_fn_eng}")
            A("sync", lambda e, slot=slot: e.dma_start(out=dram_ap, in_=qst[slot][0:rows, :]),
              waits=[f"spw{i}"], sig=tag, chan=f"d_sp{slot}", dma=True)
            spill_free[slot] = tag

        tq2 = [tq, sb("tq_b", [128, 192])]

        def xload(t):
            if t >= NT:
                return
            w = [f"xn{t - 3}"] if t >= 3 else []
            ld("scalar", xt[t % 3][:, 0:512], xv[t][:, 0:512], f"xlda{t}", f"d_xta{t % 3}", waits=w)
            ld("sync", xt[t % 3][:, 512:1024], xv[t][:, 512:1024], f"xldb{t}", f"d_xtb{t % 3}", waits=w)

        def norm_stage(B):
            def xn_op(t):
                A("vector", lambda e, t=t: e.tensor_scalar(out=xn[t % 8][:], in0=xt[t % 3], scalar1=rstd[:, t:t + 1],
                                                           scalar2=None, op0=ALU.mult),
                  waits=[f"rs{t}", f"sq{t}", f"xlda{t}", f"xldb{t}"], sig=f"xn{t}")
                xload(t + 3)
            for i in range(4):
                t = 4 * B + i
                A("scalar", lambda e, t=t: e.activation(out=xn[t % 8], in_=xt[t % 3], func=ACTF.Square,
                                                        accum_out=ss[:, t:t + 1]),
                  waits=[f"xlda{t}", f"xldb{t}", (f"tpd{B - 2}" if B >= 2 else None)], sig=f"sq{t}")
                A("vector", lambda e, t=t: e.tensor_scalar(out=rstd[:, t:t + 1], in0=ss[:, t:t + 1], scalar1=1.0 / D,
                                                           scalar2=EPS, op0=ALU.mult, op1=ALU.add),
                  waits=[f"sq{t}"], sig=f"rs0_{t}")
                A("gpsimd", lambda e, t=t: e.tensor_tensor(out=rstd[:, t:t + 1], in0=rstd[:, t:t + 1], in1=NH(rstd[:, t:t + 1]), op=ALU.pow),
                  waits=[f"rs0_{t}"], sig=f"rs{t}")
                if i >= 1:
                    xn_op(t - 1)
            xn_op(4 * B + 3)

        def tp_stage(B):
            hb = hT[B % 2]
            for fc in range(8):
                g = B * 8 + fc
                tpb = TPb[g % 2]
                for i in range(4):
                    t = 4 * B + i
                    A("tensor", lambda e, t=t, fc=fc, i=i, tpb=tpb: e.transpose(
                        tpb[:, i * 128:(i + 1) * 128], xn[t % 8][:, fc * 128:(fc + 1) * 128], ident_bf[:]),
                      waits=[f"xn{t}", CONST_ALL, "mod_ev2", (f"ht{g - 2}" if g >= 2 else None)],
                      sig=(f"tp{g}" if i == 3 else None), chan="pe_tp")
                A("vector", lambda e, fc=fc, hb=hb, tpb=tpb: e.tensor_scalar(
                    out=hb[:, fc, :], in0=tpb, scalar1=aT[:, fc:fc + 1], scalar2=shiftT[:, fc:fc + 1],
                    op0=ALU.mult, op1=ALU.add),
                  waits=[f"tp{g}", "aT", "mod_red0", (f"proj{B - 2}" if B >= 2 else None)], sig=f"ht{g}")
            sch.ev[f"tpd{B}"] = sch.ev[f"tp{B * 8 + 7}"]

        def fm_stage(B):
            hb = hT[B % 2]
            HT = f"ht{B * 8 + 7}"
            cs = slice(B * 512, (B + 1) * 512)
            fm_specs = [("qa0", 0, 64), ("qa1", 64, 64), ("ka0", 128, 64), ("ka1", 192, 64), ("ga", 256, 128), ("gb", 384, 128)]
            for name, c0, M in fm_specs:
                k = fm_i[0]
                fm_i[0] += 1
                fps = SP[0][0:M, (k % 2) * 512:(k % 2) * 512 + 512]
                for kc in range(8):
                    A("tensor", lambda e, kc=kc, c0=c0, M=M, fps=fps, hb=hb: e.matmul(
                        fps, lhsT=Win[:, kc, c0:c0 + M], rhs=hb[:, kc, :], start=(kc == 0), stop=(kc == 7)),
                      waits=[HT, "win7", (f"fme{k - 2}" if (k >= 2 and kc == 0) else None)],
                      sig=(f"fm{k}" if kc == 7 else None), chan="pe_fm")
                if name.startswith("qa"):
                    m = int(name[2])
                    spill("scalar", lambda e, dst, fps=fps: e.activation(out=dst, in_=fps, func=ACTF.Copy),
                          64, qa_scr[m][:, cs], [f"fm{k}"], f"qasp{m}_{B}")
                    sch.ev[f"fme{k}"] = sch.ev[f"spw{spill_i[0] - 1}"]
                elif name.startswith("ka"):
                    m = int(name[2])
                    A("scalar", lambda e, m=m, fps=fps, cs=cs: e.activation(out=KA[m][0:64, cs], in_=fps, func=ACTF.Copy),
                      waits=[f"fm{k}"], sig=f"fme{k}")
                else:
                    scr = sga_scr if name == "ga" else sgb_scr
                    spill("scalar", lambda e, dst, fps=fps: e.activation(out=dst, in_=fps, func=ACTF.Silu),
                          128, scr[:, cs], [f"fm{k}"], f"{name}sp_{B}")
                    sch.ev[f"fme{k}"] = sch.ev[f"spw{spill_i[0] - 1}"]

        def tm_stage(B):
            hb = hT[B % 2]
            HT = f"ht{B * 8 + 7}"
            cs = slice(B * 512, (B + 1) * 512)

            def tm_mm(i):
                t = 4 * B + i
                tps = SP[1][:, (t % 2) * 512:(t % 2) * 512 + 384]
                for kc in range(8):
                    A("tensor", lambda e, kc=kc, i=i, tps=tps: e.matmul(
                        tps, lhsT=hb[:, kc, i * 128:(i + 1) * 128], rhs=Win[:, kc, 512:896],
                        start=(kc == 0), stop=(kc == 7)),
                      waits=[HT, "win7", (f"tme{t - 2}" if (t >= 2 and kc == 0) else None)],
                      sig=(f"tm{t}" if kc == 7 else None), chan="pe_tm")
                if i == 3:
                    sch.ev[f"proj{B}"] = sch.ev[f"tm{t}"]

            def tm_evac(i):
                t = 4 * B + i
                tps = SP[1][:, (t % 2) * 512:(t % 2) * 512 + 384]
                tqt = tq2[t % 2]
                A("sync", lambda e, t=t: e.dma_start(out=ropet[t % 3][:], in_=rope_in[t]),
                  waits=[f"rp{t - 3}" if t >= 3 else None], sig=f"ropeld{t}", chan=f"d_rope{t % 3}", dma=True)
                A("scalar", lambda e, tps=tps, tqt=tqt: e.activation(out=tqt[:], in_=tps[:, 0:192], func=ACTF.Copy),
                  waits=[f"tm{t}", (f"rp{t - 2}" if t >= 2 else None)], sig=f"tqe{t}")
                A("scalar", lambda e, t=t, tps=tps: e.activation(out=VB[:, t, :], in_=tps[:, 192:256], func=ACTF.Copy))
                A("scalar", lambda e, t=t, tps=tps: e.activation(out=VA[:, t, :], in_=tps[:, 256:384], func=ACTF.Copy),
                  sig=f"tme{t}")
                A("vector", lambda e, tqt=tqt: e.tensor_tensor(out=tsq[:], in0=tqt[:], in1=tqt[:], op=ALU.mult),
                  waits=[f"tqe{t}"], sig=f"r0_{t}")
                A("vector", lambda e: e.tensor_reduce(out=ss3[:], in_=tsq[:].rearrange("p (h d) -> p h d", h=3),
                                                      axis=AX.X, op=ALU.add), waits=[f"r0_{t}"], sig=f"r1_{t}")
                A("vector", lambda e: e.tensor_scalar(out=rstd3[:], in0=ss3[:], scalar1=1.0 / 64, scalar2=EPS,
                                                      op0=ALU.mult, op1=ALU.add), waits=[f"r1_{t}"], sig=f"r2_{t}")
                A("gpsimd", lambda e: e.tensor_tensor(out=rstd3[:], in0=rstd3[:], in1=NH(rstd3[:]), op=ALU.pow),
                  waits=[f"r2_{t}"], sig=f"r3_{t}")
                for h in range(3):
                    A("vector", lambda e, h=h, tqt=tqt: e.scalar_tensor_tensor(
                        out=ty[:, h * 64:(h + 1) * 64], in0=tqt[:, h * 64:(h + 1) * 64], scalar=rstd3[:, h:h + 1],
                        in1=WN[:, h * 64:(h + 1) * 64], op0=ALU.mult, op1=ALU.mult),
                      waits=[f"r3_{t}", CONST_ALL], sig=f"r4_{t}_{h}")
                rp = ropet[t % 3]
                for h in range(3):
                    yh = ty[:, h * 64:(h + 1) * 64].rearrange("p (a s d) -> p a s d", a=2, s=2)
                    Bh = tB[:, h * 64:(h + 1) * 64].rearrange("p (a s d) -> p a s d", a=2, s=2)
                    sn = rp[:, 64:128].rearrange("p (a s d) -> p a s d", a=2, s=2)
                    A("vector", lambda e, h=h, rp=rp: e.tensor_tensor(out=tA[:, h * 64:(h + 1) * 64], in0=ty[:, h * 64:(h + 1) * 64],
                                                                      in1=rp[:, 0:64], op=ALU.mult),
                      waits=[f"r4_{t}_2", f"ropeld{t}"])
                    A("vector", lambda e, yh=yh, Bh=Bh, sn=sn: e.tensor_tensor(out=Bh[:, :, 0, :], in0=yh[:, :, 1, :],
                                                                               in1=sn[:, :, 0, :], op=ALU.mult))
                    A("vector", lambda e, yh=yh, Bh=Bh, sn=sn: e.tensor_tensor(out=Bh[:, :, 1, :], in0=yh[:, :, 0, :],
                                                                               in1=sn[:, :, 1, :], op=ALU.mult),
                      sig=f"r5_{t}_{h}")
                yot = yo[t % 2]
                A("vector", lambda e, yot=yot: e.tensor_tensor(out=yot[:, 0:192], in0=tA[:], in1=tB[:], op=ALU.add),
                  waits=[f"r5_{t}_2", (f"t2_{t - 2}" if t >= 2 else None)])
                A("vector", lambda e, yot=yot: e.tensor_tensor(out=yot[:, 192:256], in0=tA[:, 128:192], in1=tB[:, 128:192], op=ALU.add),
                  sig=f"rp{t}")

            def t2_ops(i):
                t = 4 * B + i
                yot = yo[t % 2]
                t2q = T2b[t % 2][:, 0:128]
                t2k = T2b[t % 2][:, 128:256]
                A("tensor", lambda e, yot=yot, t2q=t2q: e.transpose(t2q, yot[:, 0:128], ident_bf[:]),
                  waits=[f"rp{t}", (f"t2e{t - 2}" if t >= 2 else None)], chan="pe_t2")
                A("tensor", lambda e, yot=yot, t2k=t2k: e.transpose(t2k, yot[:, 128:256], ident_bf[:]),
                  sig=f"t2_{t}", chan="pe_t2")
                A("scalar", lambda e, t=t, i=i, t2q=t2q, B=B: e.activation(
                    out=QBst[B % 2][:, i * 128:(i + 1) * 128], in_=t2q, func=ACTF.Copy),
                  waits=[f"t2_{t}", (f"qbsp_{B - 2}" if (B >= 2 and i == 0) else None)], sig=f"t2q{t}")
                A("scalar", lambda e, t=t, t2k=t2k: e.activation(out=KB[:, t * 128:(t + 1) * 128], in_=t2k, func=ACTF.Copy),
                  waits=[f"t2_{t}"], sig=f"t2e{t}")

            tm_mm(0)
            tm_evac(0)
            for i in range(4):
                if i + 1 < 4:
                    tm_mm(i + 1)
                    tm_evac(i + 1)
                t2_ops(i)
            A("sync", lambda e, B=B, cs=cs: e.dma_start(out=qb_scr[:, cs], in_=QBst[B % 2][:]),
              waits=[f"t2q{4 * B + 3}"], sig=f"qbsp_{B}", chan=f"d_qbsp{B % 2}", dma=True)

        for t_ in range(3):
            xload(t_)
        norm_stage(0)
        tp_stage(0)
        for B in range(NB):
            if B + 1 < NB:
                norm_stage(B + 1)
            fm_stage(B)
            if B + 1 < NB:
                tp_stage(B + 1)
            tm_stage(B)
            sch.mark(f"p1_{B}")

        P1_DONE_ACT = f"t2e{NT - 1}"
        P1B = [P1_DONE_ACT, f"rp{NT - 1}", f"t2_{NT - 1}", f"xn{NT - 1}", "wout7", f"qbsp_{NB - 2}", f"qbsp_{NB - 1}"] + list(spill_free.values())

        pid = None
        g_i = [0]
        pass_i = [0]

        def block_loads(B):
            p = B % 2
            cs = slice(B * 512, (B + 1) * 512)
            prev = f"blkdone{B - 2}" if B >= 2 else None
            for m in range(2):
                ld("sync", Qb[p][m][0:64, :], qa_scr[m][:, cs], f"lq{B}_{m}b", f"d_blk{p}", waits=[prev, f"qasp{m}_{B}"] + (P1B if B < 2 else []))
                ld("sync", Qb[p][m][64:68, :], qaugb_in[:, cs], f"lq{B}_{m}bb", f"d_blk{p}")
                ld("sync", Qa[p][m][0:64, :], qa_scr[m][:, cs], f"lq{B}_{m}a", f"d_blk{p}")
                ld("sync", Qa[p][m][64:68, :], qauga_in[:, cs], f"lq{B}_{m}aa", f"d_blk{p}")
            for h in range(2):
                ld("sync", QBb[p][h][0:64, :], qb_scr[h * 64:(h + 1) * 64, cs], f"lqb{B}_{h}", f"d_blk{p}", waits=[f"qbsp_{B}"])
                ld("sync", QBb[p][h][64:128, :], qb_scr[h * 64:(h + 1) * 64, cs], f"lqb{B}_{h}d", f"d_blk{p}")
                ld("sync", SGB[p][h][:], sgb_scr[h * 64:(h + 1) * 64, cs], f"lsgb{B}_{h}", f"d_blk{p}", waits=[f"gbsp_{B}"])
            ld("sync", SGA[p][:], sga_scr[:, cs], f"blkld{B}", f"d_blk{p}", waits=[f"gasp_{B}"])

        def qk_ops(B, mp, c, dst):
            p = B % 2
            ops = []
            kc = slice(c * 128, (c + 1) * 128)
            if mp < 2:
                m = mp
                if c < 4 * B:
                    ops.append(lambda e: e.matmul(dst, lhsT=KA[m][0:68, kc], rhs=Qb[p][m][:, :], start=True, stop=True))
                elif c > 4 * B + 3:
                    ops.append(lambda e: e.matmul(dst, lhsT=KA[m][0:68, kc], rhs=Qa[p][m][:, :], start=True, stop=True))
                else:
                    j = c - 4 * B
                    if j > 0:
                        ops.append(lambda e: e.matmul(dst[:, 0:128 * j], lhsT=KA[m][0:68, kc], rhs=Qa[p][m][:, 0:128 * j],
                                                      start=True, stop=True))
                    ops.append(lambda e: e.matmul(dst[:, 128 * j:128 * (j + 1)], lhsT=KA[m][0:64, kc],
                                                  rhs=Qb[p][m][0:64, 128 * j:128 * (j + 1)], start=True, stop=False))
                    ops.append(lambda e: e.matmul(dst[:, 128 * j:128 * (j + 1)], lhsT=ident_bf[:], rhs=dtile[:],
                                                  start=False, stop=True))
                    if j < 3:
                        ops.append(lambda e: e.matmul(dst[:, 128 * (j + 1):512], lhsT=KA[m][0:68, kc],
                                                      rhs=Qb[p][m][:, 128 * (j + 1):512], start=True, stop=True))
            else:
                h = mp - 2
                r0 = 64 * (c % 2)
                ops.append(lambda e: e.matmul(dst, lhsT=KB[r0:r0 + 64, kc], rhs=QBb[p][h][r0:r0 + 64, :], start=True, stop=True))
            return ops

        def p3_pe(qb, extra=()):
            p = qb % 2
            for half in range(2):
                for ch in range(8):
                    A("tensor", lambda e, half=half, ch=ch, p=p: e.matmul(
                        YPS[:, half * 512:(half + 1) * 512], lhsT=Gg[p][:, ch, :], rhs=Wout[:, ch, half * 512:(half + 1) * 512],
                        start=(ch == 0), stop=(ch == 7)),
                      waits=[f"gg{qb}", "wout7", (f"z0_{qb - 1}" if qb >= 1 else "mod_ev2")] + list(extra),
                      sig=(f"y{qb}" if (half == 1 and ch == 7) else None), chan="pe_p3")

        def p3_dve(qb):
            p = qb % 2
            A("vector", lambda e: e.tensor_tensor(out=zt[:], in0=YPS[:], in1=gate_bc[:], op=ALU.mult),
              waits=[f"y{qb}", "mod_ev2"], sig=f"z0_{qb}")
            A("vector", lambda e, p=p: e.tensor_tensor(out=zt[:], in0=zt[:], in1=xot[p][:], op=ALU.add),
              waits=[f"z0_{qb}", f"xo{qb}"], sig=f"z1_{qb}")
            A("vector", lambda e: e.tensor_tensor(out=ot[qb % 2], in0=zt, in1=zt, op=ALU.mult), waits=[f"z1_{qb}", (f"st{qb - 2}" if qb >= 2 else None)], sig=f"z2_{qb}")
            A("vector", lambda e: e.tensor_reduce(out=zss[:, 0:1], in_=ot[qb % 2], axis=AX.X, op=ALU.add), waits=[f"z2_{qb}"], sig=f"z3_{qb}")
            A("vector", lambda e: e.tensor_scalar(out=zss[:, 1:2], in0=zss[:, 0:1], scalar1=1.0 / D, scalar2=EPS,
                                                  op0=ALU.mult, op1=ALU.add), waits=[f"z3_{qb}"], sig=f"z4_{qb}")
            A("gpsimd", lambda e: e.tensor_tensor(out=zss[:, 1:2], in0=zss[:, 1:2], in1=NH(zss[:, 1:2]), op=ALU.pow),
              waits=[f"z4_{qb}"], sig=f"z5_{qb}")
            A("vector", lambda e, p=p: e.scalar_tensor_tensor(out=ot[p][:], in0=zt[:], scalar=zss[:, 1:2], in1=fnw[:],
                                                              op0=ALU.mult, op1=ALU.mult),
              waits=[f"z5_{qb}", CONST_ALL, (f"st{qb - 2}" if qb >= 2 else None)], sig=f"z6_{qb}")
            A("sync", lambda e, p=p: e.dma_start(out=out[qb * 128:(qb + 1) * 128, :], in_=ot[p][:]),
              waits=[f"z6_{qb}"], sig=f"st{qb}", chan=f"d_st{p}", dma=True)

        def p3_loads(qb):
            p = qb % 2

            def gather(e, qb=qb, p=p):
                j = sch.pidj
                src = exout[qb].ap().rearrange("(c f) (t q) -> t f c q", f=128, q=128)
                return e.dma_start(out=Gg[p][:], in_=src[bass.ds(j, 1)].rearrange("o f c q -> (o f) c q"))
            A("sync", gather, waits=[f"cc{qb}", (f"y{qb - 2}" if qb >= 2 else None)], sig=f"gg{qb}", chan=f"d_gg{p}", dma=True)
            ld("sync", xot[p][:], xo[qb * 128:(qb + 1) * 128, :], f"xo{qb}", f"d_xo{p}", waits=[f"z1_{qb - 2}" if qb >= 2 else None])

        block_loads(0)
        block_loads(1)
        deferred = []
        last_fin_read = ["mod_ev2"]
        last_z0 = [None]
        FIN = YPS[:, 0:512]
        for B in range(NB):
            p = B % 2
            for mp in range(4):
                ps_i = pass_i[0]
                pass_i[0] += 1
                isA = mp < 2
                M = 128 if isA else 64
                gl = []
                for cp in range(32):
                    g = g_i[0]
                    g_i[0] += 1
                    gl.append(g)
                for idx in range(33):
                    for dd in [d_ for d_ in deferred if d_[0] == idx]:
                        dd[1]()
                        deferred.remove(dd)
                    if idx < 32:
                        g = gl[idx]
                        spd = SP[g % 2]
                        first = True
                        for sub in range(2):
                            c = 2 * idx + sub
                            ops = qk_ops(B, mp, c, spd[:, sub * 512:(sub + 1) * 512])
                            for oi, fn in enumerate(ops):
                                lastop = (sub == 1 and oi == len(ops) - 1)
                                w = []
                                if first:
                                    w = [f"exp{g - 2}" if g >= 2 else None, f"blkld{B}", P1_DONE_ACT, CONST_ALL]
                                    first = False
                                A("tensor", fn, waits=w, sig=(f"qk{g}" if lastop else None), chan="pe_qk")
                        A("scalar", lambda e, g=g, spd=spd: e.activation(out=PT[g % 3][:], in_=spd[:], func=ACTF.Exp, scale=0.125),
                          waits=[f"qk{g}", (f"pv{g - 3}" if g >= 3 else None)], sig=f"exp{g}", chan="act_exp")
                    if idx >= 1:
                        g = gl[idx - 1]
                        pt = PT[g % 3]
                        for sub in range(2):
                            c = 2 * (idx - 1) + sub
                            st = (c == 0)
                            sp_ = (c == NT - 1)
                            Vl = VA[:, c, :] if isA else VB[:, c, :]
                            w = [f"exp{g}"] if sub == 0 else []
                            if st:
                                w.append(f"evac{ps_i - 1}" if ps_i >= 1 else None)
                            A("tensor", lambda e, Vl=Vl, pt=pt, sub=sub, st=st, sp_=sp_, M=M: e.matmul(
                                ACCO[0:M, :], lhsT=Vl, rhs=pt[:, sub * 512:(sub + 1) * 512], start=st, stop=sp_), waits=w)
                        for sub in range(2):
                            c = 2 * (idx - 1) + sub
                            st = (c < 2)
                            sp_ = (c >= NT - 2)
                            A("tensor", lambda e, pt=pt, sub=sub, st=st, sp_=sp_: e.matmul(
                                ACCS[32 * sub:32 * sub + 1, :], lhsT=ones_bf[:, 0:1], rhs=pt[:, sub * 512:(sub + 1) * 512],
                                start=st, stop=sp_, tile_position=(0, 32 * sub)),
                              sig=(f"pv{g}" if sub == 1 else None), chan="pe_pv")
                LASTPV = f"pv{gl[-1]}"
                A("vector", lambda e, M=M: e.tensor_copy(out=Ocp[0:M, :], in_=ACCO[0:M, :]), waits=[LASTPV])
                A("vector", lambda e: e.tensor_copy(out=s64[0:1, :], in_=ACCS[0:1, :]))
                A("vector", lambda e: e.tensor_copy(out=s64[32:33, :], in_=ACCS[32:33, :]), sig=f"evac{ps_i}")
                if mp == 0 and B >= 1:
                    p3_pe(B - 1, [last_fin_read[0]])
                    p3_dve(B - 1)
                z0dep = f"z0_{B - 1}" if (B >= 1 and mp == 0) else last_z0[0]
                if B >= 1 and mp == 0:
                    last_z0[0] = f"z0_{B - 1}"
                deferred.append((2, lambda M=M, ps_i=ps_i, z0dep=z0dep: A(
                    "tensor", lambda e: e.matmul(FIN[0:M, :], lhsT=sel[:, 0:M], rhs=s64[:, :], start=True, stop=True),
                    waits=[f"evac{ps_i}", CONST_ALL, z0dep], sig=f"sbc{ps_i}", chan="pe_fin")))
                A("vector", lambda e, M=M: e.reciprocal(out=fR[0:M, :], in_=FIN[0:M, :]), waits=[f"sbc{ps_i}"], sig=f"f0_{ps_i}")
                last_fin_read[0] = f"f0_{ps_i}"
                if mp == 0:
                    A("vector", lambda e: e.tensor_tensor(out=fT1[:], in0=Ocp[:], in1=fR[:], op=ALU.mult),
                      waits=[f"f0_{ps_i}"], sig=f"fin{ps_i}")
                elif mp == 1:
                    A("vector", lambda e: e.tensor_tensor(out=fT2[:], in0=Ocp[:], in1=fR[:], op=ALU.mult),
                      waits=[f"f0_{ps_i}"], sig=f"f1_{ps_i}")
                    A("vector", lambda e: e.scalar_tensor_tensor(out=fD[:], in0=fT2[:], scalar=nlam[:, 0:1], in1=fT1[:],
                                                                 op0=ALU.mult, op1=ALU.add),
                      waits=[f"f1_{ps_i}", "nlam"], sig=f"f2_{ps_i}")
                    A("vector", lambda e: e.tensor_tensor(out=fD2[:], in0=fD[:], in1=fD[:], op=ALU.mult),
                      waits=[f"f2_{ps_i}"], sig=f"f3_{ps_i}")
                    deferred.append((8, lambda ps_i=ps_i: A(
                        "tensor", lambda e: e.matmul(FIN[:], lhsT=ones_f[:], rhs=fD2[:], start=True, stop=True),
                        waits=[f"f3_{ps_i}", "ones"], sig=f"ssn{ps_i}", chan="pe_fin")))
                    A("vector", lambda e: e.tensor_scalar(out=fRS[:], in0=FIN[:], scalar1=1.0 / 128, scalar2=EPS,
                                                          op0=ALU.mult, op1=ALU.add), waits=[f"ssn{ps_i}"], sig=f"f4_{ps_i}")
                    last_fin_read[0] = f"f4_{ps_i}"
                    deferred.append((12, lambda ps_i=ps_i: A(
                        "scalar", lambda e: e.activation(out=fRS[:], in_=fRS[:], func=ACTF.Ln), waits=[f"f4_{ps_i}"], sig=f"f4b_{ps_i}")))
                    deferred.append((12, lambda ps_i=ps_i: A(
                        "scalar", lambda e: e.activation(out=fRS[:], in_=fRS[:], func=ACTF.Exp, scale=-0.5),
                        waits=[f"f4b_{ps_i}"], sig=f"fin{ps_i}")))
                    A("vector", lambda e: e.scalar_tensor_tensor(out=fD[:], in0=fD[:], scalar=subw[:, 0:1], in1=fRS[:],
                                                                 op0=ALU.mult, op1=ALU.mult),
                      waits=[f"fin{ps_i}", "subw"], sig=f"f5_{ps_i}")
                    A("vector", lambda e, p=p: e.tensor_tensor(out=GA[p][:], in0=fD[:], in1=SGA[p][:], op=ALU.mult),
                      waits=[f"f5_{ps_i}", f"blkld{B}", (f"exw{B - 2}" if B >= 2 else None)], sig=f"ga{B}")
                else:
                    h = mp - 2
                    A("vector", lambda e: e.tensor_tensor(out=fT1[0:64, :], in0=Ocp[0:64, :], in1=fR[0:64, :], op=ALU.mult),
                      waits=[f"f0_{ps_i}"], sig=f"fin{ps_i}")
                    A("vector", lambda e, p=p, h=h: e.tensor_tensor(out=GB[p][h][:], in0=fT1[0:64, :], in1=SGB[p][h][:], op=ALU.mult),
                      waits=[f"fin{ps_i}", f"blkld{B}", (f"exw{B - 2}" if B >= 2 else None)], sig=f"gb{B}_{h}")
                if B == NB - 1 and mp == 3:
                    for _, fn_ in deferred:
                        fn_()
                    deferred.clear()
            sch.ev[f"blkdone{B}"] = sch.ev[f"gb{B}_1"]
            exi = exin[B].ap()
            A("gpsimd", lambda e, p=p, exi=exi: e.dma_start(out=exi[0:128, :], in_=GA[p][:]), waits=[f"ga{B}"],
              sig=f"exw{B}_0", chan=f"d_exw{p}", dma=True)
            A("gpsimd", lambda e, p=p, exi=exi: e.dma_start(out=exi[128:192, :], in_=GB[p][0][:]), waits=[f"gb{B}_0"],
              sig=f"exw{B}_1", chan=f"d_exw{p}", dma=True)
            A("gpsimd", lambda e, p=p, exi=exi: e.dma_start(out=exi[192:256, :], in_=GB[p][1][:]), waits=[f"gb{B}_1"],
              sig=f"exw{B}", chan=f"d_exw{p}", dma=True)
            A("gpsimd", lambda e, B=B: e.collective_compute(
                "AllGather", ALU.bypass, replica_groups=[[0, 1, 2, 3], [4, 5, 6, 7]],
                ins=[exin[B].ap()], outs=[exout[B].ap()]), waits=[f"exw{B}"], sig=f"cc{B}", chan="cc")
            sch.mark(f"p2a_{B}")
            if B + 2 < NB:
                block_loads(B + 2)
            p3_loads(B)
            sch.mark(f"p2_{B}")
        p3_pe(NB - 1, [last_fin_read[0]])
        p3_dve(NB - 1)
        A("sync", lambda e: e.nop(), waits=[f"st{NB - 2}", f"st{NB - 1}"])

        sch.limit = trunc
        chans = sorted(sch.cnt.keys())
        sems = {ch: es.enter_context(nc.semaphore(ch)) for ch in chans}
        with nc.Block() as block:
            @block.sync
            def _(e):
                sch.emit("sync", e, sems)

            @block.scalar
            def _(e):
                sch.emit("scalar", e, sems)

            @block.gpsimd
            def _(e):
                sch.emit("gpsimd", e, sems)

            @block.vector
            def _(e):
                sch.emit("vector", e, sems)

            @block.tensor
            def _(e):
                sch.emit("tensor", e, sems)
    return nc


def _host_inputs(inputs):
    f32 = np.float32
    bf = ml_dtypes.bfloat16
    x = np.asarray(inputs["x"], f32)
    c = np.asarray(inputs["c"], f32)
    w_ada = np.ascontiguousarray(np.asarray(inputs["w_ada"], f32)[0])
    b_ada = np.asarray(inputs["b_ada"], f32)[0]
    norm_w = np.asarray(inputs["norm_w"], f32)[0]
    w_in = np.asarray(inputs["w_in"], f32)[0]
    w_out = np.asarray(inputs["w_out"], f32)[0]
    lq1 = np.asarray(inputs["lambda_q1"], f32)[0]
    lk1 = np.asarray(inputs["lambda_k1"], f32)[0]
    lq2 = np.asarray(inputs["lambda_q2"], f32)[0]
    lk2 = np.asarray(inputs["lambda_k2"], f32)[0]
    subln = np.asarray(inputs["subln_w"], f32)[0]
    qw = np.asarray(inputs["q_norm_w"], f32)[0]
    kw = np.asarray(inputs["k_norm_w"], f32)[0]
    fnw = np.asarray(inputs["final_norm_w"], f32)

    def bc(v):
        return np.ascontiguousarray(np.broadcast_to(v[None, :], (128, v.shape[0])))

    tok = np.arange(S)
    row = (tok // 64).astype(f32)
    col = (tok % 64).astype(f32)
    freqs = (f32(1.0) / (f32(10000.0) ** (np.arange(16, dtype=f32) * f32(2.0) / f32(32)))).astype(f32)
    ang_r = (row[:, None] * freqs[None, :]).astype(f32)
    ang_c = (col[:, None] * freqs[None, :]).astype(f32)
    cr, sr, cc, sn_c = np.cos(ang_r), np.sin(ang_r), np.cos(ang_c), np.sin(ang_c)
    cos64 = np.concatenate([cr, cr, cc, cc], axis=1)
    sin64 = np.concatenate([-sr, sr, -sn_c, sn_c], axis=1)
    rope = np.ascontiguousarray(np.concatenate([cos64, sin64], axis=1).astype(f32).reshape(NT, 128, 128))
    ident = np.eye(128, dtype=f32)
    selm = np.zeros((64, 128), f32)
    selm[0, :] = 1.0
    selm[32, :] = 1.0
    w_out_rows = np.concatenate([np.concatenate([np.arange(r * 128, (r + 1) * 128),
                                                 512 + np.arange(r * 128, (r + 1) * 128)]) for r in range(4)])
    w_out_p = np.ascontiguousarray(w_out[w_out_rows])
    shared = dict(
        w_ada=w_ada, bada=bc(b_ada), normwT=np.ascontiguousarray(norm_w.reshape(8, 128).T),
        w_out=w_out_p, lam_in=bc(np.concatenate([lq1, lk1, lq2, lk2])), sublnT=np.ascontiguousarray(subln[:, None]),
        wn_in=bc(np.concatenate([qw, qw, kw])), fnw_in=bc(fnw), ident_bf=ident.astype(bf), ident_f=ident, rope=rope, sel=selm,
    )
    maps = []
    pos = np.arange(S)
    a_, r_ = (pos // 128).astype(f32), (pos % 128).astype(f32)
    for cid in range(8):
        b, j = cid // 4, cid % 4
        g = j // 2
        cols = np.concatenate([
            j * 128 + np.arange(128),
            512 + j * 128 + np.arange(128),
            1536 + j * 128 + np.arange(128),
            2816 + 2 * j * 64 + np.arange(128),
            2048 + 2 * j * 64 + np.arange(128),
            2560 + g * 64 + np.arange(64),
            2688 + g * 64 + np.arange(64),
            1024 + j * 128 + np.arange(128),
        ])
        sig = f32(8.0 * 2.0 ** (-2.0 * (j + 1)))
        kaug = np.stack([sig * r_, sig * 128 * a_, np.ones(S, f32), np.ones(S, f32)])
        qaugb = np.stack([np.ones(S, f32), np.ones(S, f32), -sig * 128 * a_, -sig * r_])
        ii = np.arange(128, dtype=f32)
        dt_ = -sig * np.abs(ii[None, :] - ii[:, None])
        xb = np.ascontiguousarray(x[b])
        m = dict(shared)
        m.update(
            x=xb, xo=np.ascontiguousarray(xb.reshape(NB, 4, 128, D)[:, j].reshape(NB * 128, D)),
            cT=np.ascontiguousarray(c[b].reshape(8, 128).T), w_in=np.ascontiguousarray(w_in[:, cols]),
            kaug=kaug.astype(bf), qaugb=qaugb.astype(bf), qauga=(-qaugb).astype(bf), dtile=dt_.astype(bf),
        )
        maps.append(m)
    return maps


_NC = None


def kernel(**inputs):
    global _NC
    if _NC is None:
        _NC = build_program()
    maps = _host_inputs(inputs)
    res = run_bass_kernel_spmd(_NC, maps, core_ids=list(range(8)))
    outp = np.empty((2, NB, 4, 128, D), np.float32)
    for cid in range(8):
        b, j = cid // 4, cid % 4
        outp[b, :, j] = np.asarray(res.results[cid]["out"], np.float32).reshape(NB, 128, D)
    return outp.reshape(2, S, D)
```
